# Optimizing a Trainium2 kernel written in Bass

```python
import math
import jax, jax.numpy as jnp
from jax import lax
import numpy as np

D_MODEL = 1024
BATCH = 4
SEQ = 8192
DEPTH = 4

GRID_W = 64
HEAD_DIM = D_MODEL // 16
Q_BLOCK = 128
RMS_EPS = 1e-6
NEG_INF = -1e30

NA_HEADS = 4
NA_WIN_H = 8
NA_WIN_W = 16
DIFF_HEADS = 4
DIFF_QK_DIM = HEAD_DIM // 2
DIFF_V_DIM = HEAD_DIM
GQA_Q_HEADS = 4
GQA_KV_HEADS = 2
AXIAL_THETA = 10000.0
DIL_HEADS = 4
DIL_PAIRS = ((128, 1), (512, 4), (2048, 16))
ROPE_THETA = 500000.0
ROPE_FRACTION = 4
D_FF = ((8 * D_MODEL // 3 + 255) // 256) * 256

NA_W = NA_HEADS * HEAD_DIM
DIFF_QK_W = DIFF_HEADS * 2 * DIFF_QK_DIM
DIFF_V_W = DIFF_HEADS * DIFF_V_DIM
GQA_Q_W = GQA_Q_HEADS * HEAD_DIM
GQA_KV_W = GQA_KV_HEADS * HEAD_DIM
DIL_W = DIL_HEADS * HEAD_DIM
IN_SIZES = (NA_W, NA_W, NA_W, DIFF_QK_W, DIFF_QK_W, DIFF_V_W, GQA_Q_W, GQA_KV_W, GQA_KV_W, DIL_W, DIL_W, DIL_W)
IN_WIDTH = sum(IN_SIZES)
MIX_WIDTH = NA_W + DIFF_V_W + GQA_Q_W + DIL_W

kernel_name = 'hybrid_parallel_head_encoder'


def rms_norm(x, gain):
    xf = x.astype(jnp.float32)
    y = xf * lax.rsqrt(jnp.mean(xf * xf, axis=-1, keepdims=True) + RMS_EPS)
    return (y * gain.astype(jnp.float32)).astype(x.dtype)


def rope_tables(pos, dim, theta):
    inv_freq = 1.0 / (theta ** (jnp.arange(0, dim, 2, dtype=jnp.float32) / dim))
    ang = pos.astype(jnp.float32)[:, None] * inv_freq[None, :]
    return jnp.cos(ang), jnp.sin(ang)


def apply_rope(x, cos, sin):
    x1, x2 = jnp.split(x.astype(jnp.float32), 2, axis=-1)
    c = cos[None, :, None, :]
    s = sin[None, :, None, :]
    return jnp.concatenate([x1 * c - x2 * s, x1 * s + x2 * c], axis=-1).astype(x.dtype)


def partial_rope(x, cos, sin, n_rot):
    return jnp.concatenate([apply_rope(x[..., :n_rot], cos, sin), x[..., n_rot:]], axis=-1)


def swiglu(h, w_gate, w_up, w_down):
    return (jax.nn.silu(h @ w_gate) * (h @ w_up)) @ w_down


def split_in_proj(h):
    points = np.cumsum(IN_SIZES)[:-1].tolist()
    return jnp.split(h, points, axis=-1)


def neighborhood_attention(q, k, v, q_gain, k_gain, rel_bias, rows):
    B, S, _ = q.shape
    H, d = NA_HEADS, HEAD_DIM
    q = rms_norm(q.reshape(B, S, H, d), q_gain)
    k = rms_norm(k.reshape(B, S, H, d), k_gain)
    v = v.reshape(B, S, H, d)
    to_grid = lambda t: t.reshape(B, rows, GRID_W, H, d).transpose(0, 3, 1, 2, 4)
    qg, kg, vg = to_grid(q), to_grid(k), to_grid(v)
    kh = min(NA_WIN_H, rows)
    cols = jnp.arange(GRID_W)
    col_idx = jnp.clip(cols - NA_WIN_W // 2, 0, GRID_W - NA_WIN_W)[:, None] + jnp.arange(NA_WIN_W)[None, :]
    col_off = col_idx - cols[:, None] + (NA_WIN_W - 1)
    scale = d ** -0.5

    def row_fn(r):
        rs = jnp.clip(r - kh // 2, 0, rows - kh)
        q_r = lax.dynamic_index_in_dim(qg, r, axis=2, keepdims=False)
        k_rows = lax.dynamic_slice_in_dim(kg, rs, kh, axis=2)
        v_rows = lax.dynamic_slice_in_dim(vg, rs, kh, axis=2)
        k_win = k_rows[:, :, :, col_idx]
        v_win = v_rows[:, :, :, col_idx]
        row_off = rs + jnp.arange(kh) - r + (NA_WIN_H - 1)
        bias = rel_bias[:, row_off][:, :, col_off].transpose(0, 2, 1, 3)
        s = jnp.einsum('bhcd,bhacjd->bhcaj', q_r, k_win).astype(jnp.float32) * scale
        s = s + bias[None].astype(jnp.float32)
        p = jax.nn.softmax(s.reshape(B, H, GRID_W, kh * NA_WIN_W), axis=-1).reshape(s.shape)
        return jnp.einsum('bhcaj,bhacjd->bhcd', p.astype(v.dtype), v_win)

    out = lax.map(row_fn, jnp.arange(rows))
    return out.transpose(1, 0, 3, 2, 4).reshape(B, S, H * d)


def diff_attention(q, k, v, q_gain, k_gain, lq1, lk1, lq2, lk2, out_gain, lambda_init, cos, sin):
    B, S, _ = q.shape
    H, d = DIFF_HEADS, DIFF_QK_DIM
    n_rot = d // ROPE_FRACTION

    def prep(t, g):
        t = partial_rope(rms_norm(t.reshape(B, S, 2 * H, d), g), cos, sin, n_rot)
        return t.reshape(B, S, H, 2, d).transpose(3, 0, 2, 1, 4)

    q12 = prep(q, q_gain)
    k12 = prep(k, k_gain)
    vt = v.reshape(B, S, H, DIFF_V_DIM).transpose(0, 2, 1, 3)
    lam = (jnp.exp(jnp.sum(lq1.astype(jnp.float32) * lk1.astype(jnp.float32)))
           - jnp.exp(jnp.sum(lq2.astype(jnp.float32) * lk2.astype(jnp.float32))) + lambda_init)
    scale = d ** -0.5
    nb = S // Q_BLOCK
    qb = q12.reshape(2, B, H, nb, Q_BLOCK, d).transpose(3, 0, 1, 2, 4, 5)

    def block_fn(qblk):
        s = jnp.einsum('ibhqd,ibhkd->ibhqk', qblk, k12).astype(jnp.float32) * scale
        p = jax.nn.softmax(s, axis=-1)
        a = p[0] - lam * p[1]
        return jnp.einsum('bhqk,bhkd->bhqd', a.astype(vt.dtype), vt)

    o = lax.map(block_fn, qb)
    o = o.transpose(1, 2, 0, 3, 4).reshape(B, H, S, DIFF_V_DIM)
    o = rms_norm(o, out_gain) * (1.0 - lambda_init)
    return o.transpose(0, 2, 1, 3).reshape(B, S, H * DIFF_V_DIM)


def gqa_axial_attention(q, k, v, q_gain, k_gain, row_cs, col_cs):
    B, S, _ = q.shape
    Hq, Hkv, d = GQA_Q_HEADS, GQA_KV_HEADS, HEAD_DIM
    G = Hq // Hkv
    half = d // 2

    def axial(t):
        return jnp.concatenate([apply_rope(t[..., :half], *row_cs), apply_rope(t[..., half:], *col_cs)], axis=-1)

    q = axial(rms_norm(q.reshape(B, S, Hq, d), q_gain))
    k = axial(rms_norm(k.reshape(B, S, Hkv, d), k_gain))
    kt = k.transpose(0, 2, 1, 3)
    vt = v.reshape(B, S, Hkv, d).transpose(0, 2, 1, 3)
    nb = S // Q_BLOCK
    qb = q.reshape(B, nb, Q_BLOCK, Hkv, G, d).transpose(1, 0, 3, 4, 2, 5)
    scale = d ** -0.5

    def block_fn(qblk):
        s = jnp.einsum('bngqd,bnkd->bngqk', qblk, kt).astype(jnp.float32) * scale
        p = jax.nn.softmax(s, axis=-1)
        return jnp.einsum('bngqk,bnkd->bngqd', p.astype(vt.dtype), vt)

    o = lax.map(block_fn, qb)
    return o.transpose(1, 0, 4, 2, 3, 5).reshape(B, S, Hq * d)


def dilated_attention(q, k, v, q_gain, k_gain, cos, sin):
    B, S, _ = q.shape
    H, d = DIL_HEADS, HEAD_DIM
    n_rot = d // ROPE_FRACTION
    q = partial_rope(rms_norm(q.reshape(B, S, H, d), q_gain), cos, sin, n_rot).transpose(0, 2, 1, 3)
    k = partial_rope(rms_norm(k.reshape(B, S, H, d), k_gain), cos, sin, n_rot).transpose(0, 2, 1, 3)
    v = v.reshape(B, S, H, d).transpose(0, 2, 1, 3)
    nb = S // Q_BLOCK
    scale = d ** -0.5

    def block_fn(b):
        t = b * Q_BLOCK + jnp.arange(Q_BLOCK)
        q_b = lax.dynamic_slice_in_dim(q, b * Q_BLOCK, Q_BLOCK, axis=2)
        outs, lses = [], []
        for window, dil in DIL_PAIRS:
            n_side = (window // 2) // dil
            idx = t[:, None] + dil * jnp.arange(-n_side, n_side + 1)[None, :]
            valid = (idx >= 0) & (idx < S)
            idx = jnp.clip(idx, 0, S - 1)
            k_g = jnp.take(k, idx, axis=2)
            v_g = jnp.take(v, idx, axis=2)
            s = jnp.einsum('bhqd,bhqnd->bhqn', q_b, k_g).astype(jnp.float32) * scale
            s = jnp.where(valid[None, None], s, NEG_INF)
            lse = jax.nn.logsumexp(s, axis=-1, keepdims=True)
            p = jnp.exp(s - lse)
            outs.append(jnp.einsum('bhqn,bhqnd->bhqd', p.astype(v.dtype), v_g).astype(jnp.float32))
            lses.append(lse)
        w = jax.nn.softmax(jnp.stack(lses, axis=0), axis=0)
        return jnp.sum(w * jnp.stack(outs, axis=0), axis=0).astype(v.dtype)

    o = lax.map(block_fn, jnp.arange(nb))
    return o.transpose(1, 0, 3, 2, 4).reshape(B, S, H * d)


def setup_inputs(seed: int = 0) -> dict:
    key = jax.random.key(seed)
    ks = jax.random.split(key, 32)
    L, D, F = DEPTH, D_MODEL, D_FF

    def w(i, shape, fan_in):
        return jax.random.normal(ks[i], shape, jnp.float32) * (fan_in ** -0.5)

    def gain(i, shape):
        return 1.0 + 0.02 * jax.random.normal(ks[i], shape, jnp.float32)

    def small(i, shape, sd):
        return sd * jax.random.normal(ks[i], shape, jnp.float32)

    return {
        'x': jax.random.normal(ks[0], (BATCH, SEQ, D), jnp.float32),
        'ffn1_norm': gain(1, (L, D)),
        'ffn1_w_gate': w(2, (L, D, F), D),
        'ffn1_w_up': w(3, (L, D, F), D),
        'ffn1_w_down': w(4, (L, F, D), F),
        'mix_norm': gain(5, (L, D)),
        'w_in': w(6, (L, D, IN_WIDTH), D),
        'w_out': w(7, (L, MIX_WIDTH, D), MIX_WIDTH),
        'na_q_norm': gain(8, (L, HEAD_DIM)),
        'na_k_norm': gain(9, (L, HEAD_DIM)),
        'na_rel_bias': small(10, (L, NA_HEADS, 2 * NA_WIN_H - 1, 2 * NA_WIN_W - 1), 0.02),
        'diff_q_norm': gain(11, (L, DIFF_QK_DIM)),
        'diff_k_norm': gain(12, (L, DIFF_QK_DIM)),
        'diff_lambda_q1': small(13, (L, DIFF_QK_DIM), 0.1),
        'diff_lambda_k1': small(14, (L, DIFF_QK_DIM), 0.1),
        'diff_lambda_q2': small(15, (L, DIFF_QK_DIM), 0.1),
        'diff_lambda_k2': small(16, (L, DIFF_QK_DIM), 0.1),
        'diff_out_norm': gain(17, (L, DIFF_V_DIM)),
        'gqa_q_norm': gain(18, (L, HEAD_DIM)),
        'gqa_k_norm': gain(19, (L, HEAD_DIM)),
        'dil_q_norm': gain(20, (L, HEAD_DIM)),
        'dil_k_norm': gain(21, (L, HEAD_DIM)),
        'ffn2_norm': gain(22, (L, D)),
        'ffn2_w_gate': w(23, (L, D, F), D),
        'ffn2_w_up': w(24, (L, D, F), D),
        'ffn2_w_down': w(25, (L, F, D), F),
    }


def reference(x, ffn1_norm, ffn1_w_gate, ffn1_w_up, ffn1_w_down, mix_norm, w_in, w_out,
              na_q_norm, na_k_norm, na_rel_bias, diff_q_norm, diff_k_norm,
              diff_lambda_q1, diff_lambda_k1, diff_lambda_q2, diff_lambda_k2, diff_out_norm,
              gqa_q_norm, gqa_k_norm, dil_q_norm, dil_k_norm,
              ffn2_norm, ffn2_w_gate, ffn2_w_up, ffn2_w_down):
    B, S, _ = x.shape
    rows = S // GRID_W
    pos = jnp.arange(S, dtype=jnp.int32)
    diff_cs = rope_tables(pos, DIFF_QK_DIM // ROPE_FRACTION, ROPE_THETA)
    dil_cs = rope_tables(pos, HEAD_DIM // ROPE_FRACTION, ROPE_THETA)
    row_cs = rope_tables(pos // GRID_W, HEAD_DIM // 2, AXIAL_THETA)
    col_cs = rope_tables(pos % GRID_W, HEAD_DIM // 2, AXIAL_THETA)

    for l in range(DEPTH):
        h = rms_norm(x, ffn1_norm[l])
        x = x + 0.5 * swiglu(h, ffn1_w_gate[l], ffn1_w_up[l], ffn1_w_down[l])

        h = rms_norm(x, mix_norm[l])
        (na_q, na_k, na_v, df_q, df_k, df_v,
         gq_q, gq_k, gq_v, dl_q, dl_k, dl_v) = split_in_proj(h @ w_in[l])
        o_a = neighborhood_attention(na_q, na_k, na_v, na_q_norm[l], na_k_norm[l], na_rel_bias[l], rows)
        lambda_init = 0.8 - 0.6 * math.exp(-0.3 * l)
        o_b = diff_attention(df_q, df_k, df_v, diff_q_norm[l], diff_k_norm[l],
                             diff_lambda_q1[l], diff_lambda_k1[l], diff_lambda_q2[l], diff_lambda_k2[l],
                             diff_out_norm[l], lambda_init, *diff_cs)
        o_c = gqa_axial_attention(gq_q, gq_k, gq_v, gqa_q_norm[l], gqa_k_norm[l], row_cs, col_cs)
        o_d = dilated_attention(dl_q, dl_k, dl_v, dil_q_norm[l], dil_k_norm[l], *dil_cs)
        o = jnp.concatenate([o_a, o_b, o_c, o_d], axis=-1)
        x = x + o @ w_out[l]

        h = rms_norm(x, ffn2_norm[l])
        x = x + 0.5 * swiglu(h, ffn2_w_gate[l], ffn2_w_up[l], ffn2_w_down[l])
    return x
```

```python
import math
import numpy as np
import concourse.bass as bass
import concourse.mybir as mybir
from concourse.bass_utils import run_bass_kernel_spmd

F32 = mybir.dt.float32
BF16 = mybir.dt.bfloat16
AF = mybir.ActivationFunctionType
ALU = mybir.AluOpType

D = 1024
F = 2816
NFC = 22
L = 4
S = 8192
TOK = 4096
G = 512
NG = TOK // G
EPS = 1e-6
NEG = -30000.0
NV = 40
SLOT = 3584
NSLOT = 6

C_NAQ, C_NAK, C_NAV, C_DFQ, C_DFK, C_DFV, C_GQQ, C_GQK, C_GQV, C_DLQ, C_DLK, C_DLV = (
    0, 256, 512, 768, 1024, 1280, 1536, 1792, 1920, 2048, 2304, 2560)
QK_CHUNKS = [(C_NAQ, "na", 0), (C_NAQ + 128, "na", 0), (C_DFQ, "diff", 0), (C_DFQ + 128, "diff", 0),
             (C_GQQ, "gqa", 0), (C_GQQ + 128, "gqa", 0), (C_DLQ, "dil", 0), (C_DLQ + 128, "dil", 0),
             (C_NAK, "na", 1), (C_NAK + 128, "na", 1), (C_DFK, "diff", 1), (C_DFK + 128, "diff", 1),
             (C_GQK, "gqa", 1), (C_DLK, "dil", 1), (C_DLK + 128, "dil", 1)]
HD = {"na": 64, "diff": 32, "gqa": 64, "dil": 64}
GCOL = {("na", 0): (24, None), ("na", 1): (25, None), ("diff", 0): (26, 27), ("diff", 1): (28, 29),
        ("gqa", 0): (30, 31), ("gqa", 1): (32, 33), ("dil", 0): (34, 35), ("dil", 1): (36, 37)}
TABI = {"diff": 0, "gqa": 2, "dil": 4}
V_COLS = [(C_NAV, 256), (C_DFV, 256), (C_GQV, 128), (C_DLV, 256)]


def swap_index(typ):
    d = HD[typ]
    idx = np.arange(d)
    if typ == "gqa":
        jj = idx % 32
        return (idx // 32) * 32 + (jj + 16) % 32
    if typ == "diff":
        out = idx.copy()
        out[:8] = (idx[:8] + 4) % 8
        return out
    if typ == "dil":
        out = idx.copy()
        out[:16] = (idx[:16] + 8) % 16
        return out
    return idx


def layer_tile_table():
    tab = {}
    off = 0

    def add(name, i, n):
        nonlocal off
        tab[(name, i)] = (off, n)
        off += n
    for ff in ("f1", "f2"):
        for fc in range(NFC):
            add(ff + "gu", fc, 2048)
        for c in range(8):
            add(ff + "d", c, 2816)
    for ci, (_, typ, _) in enumerate(QK_CHUNKS):
        add("qk", ci, 1024 if typ == "na" else 2048)
    for i in range(2):
        add("v", i, 3584)
    for i in range(4):
        add("wo", i, 2048)
    return tab, off


TILE_TAB, LAYER_W = layer_tile_table()


def lhsT_tile(w, c0, ncols=128):
    return w[:, c0:c0 + ncols].reshape(8, 128, ncols).transpose(1, 0, 2)


def prep_weights(inp):
    wall = np.empty((128, L * LAYER_W), np.float32)
    for l in range(L):
        base = l * LAYER_W
        for ff, (ng, nu, nd) in (("f1", ("ffn1_w_gate", "ffn1_w_up", "ffn1_w_down")),
                                 ("f2", ("ffn2_w_gate", "ffn2_w_up", "ffn2_w_down"))):
            wg, wu, wd = inp[ng][l], inp[nu][l], inp[nd][l]
            for fc in range(NFC):
                o, n = TILE_TAB[(ff + "gu", fc)]
                t = np.stack([lhsT_tile(wg, fc * 128), lhsT_tile(wu, fc * 128)], axis=1)
                wall[:, base + o:base + o + n] = t.reshape(128, n)
            wd3 = wd.reshape(NFC, 128, D)
            for c in range(8):
                o, n = TILE_TAB[(ff + "d", c)]
                t = wd3[:, :, c * 128:(c + 1) * 128].transpose(1, 0, 2)
                wall[:, base + o:base + o + n] = t.reshape(128, n)
        w_in = inp["w_in"][l]
        for ci, (c0, typ, _) in enumerate(QK_CHUNKS):
            o, n = TILE_TAB[("qk", ci)]
            main = lhsT_tile(w_in, c0)
            if typ == "na":
                wall[:, base + o:base + o + n] = main.reshape(128, n)
            else:
                d = HD[typ]
                sw = swap_index(typ)
                cols = c0 + (np.arange(128) // d) * d + sw[np.arange(128) % d]
                swp = w_in[:, cols].reshape(8, 128, 128).transpose(1, 0, 2)
                wall[:, base + o:base + o + n] = np.stack([main, swp], axis=1).reshape(128, n)
        wv = np.concatenate([w_in[:, c0:c0 + n_] for c0, n_ in V_COLS], axis=1)
        wv3 = wv.reshape(8, 128, 896).transpose(1, 0, 2)
        for i in range(2):
            o, n = TILE_TAB[("v", i)]
            wall[:, base + o:base + o + n] = wv3[:, :, i * 448:(i + 1) * 448].reshape(128, n)
        wo = inp["w_out"][l]
        for i in range(4):
            o, n = TILE_TAB[("wo", i)]
            t = np.stack([lhsT_tile(wo, (2 * i) * 128), lhsT_tile(wo, (2 * i + 1) * 128)], axis=1)
            wall[:, base + o:base + o + n] = t.reshape(128, n)
    return wall


def prep_vecs(inp):
    vecs = np.zeros((128, L * NV), np.float32)
    p = np.arange(128)
    for l in range(L):
        b = l * NV
        vecs[:, b + 0:b + 8] = inp["ffn1_norm"][l].reshape(8, 128).T
        vecs[:, b + 8:b + 16] = inp["mix_norm"][l].reshape(8, 128).T
        vecs[:, b + 16:b + 24] = inp["ffn2_norm"][l].reshape(8, 128).T
        names = {("na", 0): "na_q_norm", ("na", 1): "na_k_norm", ("diff", 0): "diff_q_norm",
                 ("diff", 1): "diff_k_norm", ("gqa", 0): "gqa_q_norm", ("gqa", 1): "gqa_k_norm",
                 ("dil", 0): "dil_q_norm", ("dil", 1): "dil_k_norm"}
        for key, nm in names.items():
            g = inp[nm][l]
            d = HD[key[0]]
            c_main, c_sw = GCOL[key]
            vecs[:, b + c_main] = g[p % d]
            if c_sw is not None:
                vecs[:, b + c_sw] = g[swap_index(key[0])[p % d]]
        vecs[:, b + 38] = inp["diff_out_norm"][l][p % 64]
    lam = np.zeros((64, L * 128), np.float32)
    for l in range(L):
        for i, nm in enumerate(("diff_lambda_q1", "diff_lambda_k1", "diff_lambda_q2", "diff_lambda_k2")):
            lam[:, l * 128 + i * 32:l * 128 + (i + 1) * 32] = np.broadcast_to(inp[nm][l][None, :], (64, 32))
    rb = inp["na_rel_bias"]
    kc = np.arange(64)[:, None]
    qc = np.arange(64)[None, :]
    co = np.clip(kc - qc + 15, 0, 30)
    natab = np.zeros((L, 4, 23, 64, 64), np.float32)
    for a in range(23):
        dr = 18 - a
        if 0 <= dr <= 14:
            natab[:, :, a] = rb[:, :, dr][:, :, co]
    return vecs, lam, natab


def rope_tables_core(half):
    pos = (half * TOK + np.arange(TOK)).astype(np.int32)
    out = np.zeros((128, 6, TOK), np.float32)
    p = np.arange(128)

    def tables(posv, dim, theta):
        inv = (1.0 / (np.float32(theta) ** (np.arange(0, dim, 2, dtype=np.float32) / np.float32(dim)))).astype(np.float32)
        ang = posv.astype(np.float32)[:, None] * inv[None, :]
        return np.cos(ang).astype(np.float32), np.sin(ang).astype(np.float32)
    c, s = tables(pos, 8, 500000.0)
    j = p % 32
    for pp in range(128):
        jj = j[pp]
        if jj < 8:
            out[pp, 0] = c[:, jj % 4]
            out[pp, 1] = (-s[:, jj % 4]) if jj < 4 else s[:, jj % 4]
        else:
            out[pp, 0] = 1.0
    cr, sr = tables(pos // 64, 32, 10000.0)
    cc, sc = tables(pos % 64, 32, 10000.0)
    j = p % 64
    for pp in range(128):
        jj = j[pp] % 32
        cT, sT = (cr, sr) if j[pp] < 32 else (cc, sc)
        out[pp, 2] = cT[:, jj % 16]
        out[pp, 3] = (-sT[:, jj % 16]) if jj < 16 else sT[:, jj % 16]
    c, s = tables(pos, 16, 500000.0)
    for pp in range(128):
        jj = j[pp]
        if jj < 16:
            out[pp, 4] = c[:, jj % 8]
            out[pp, 5] = (-s[:, jj % 8]) if jj < 8 else s[:, jj % 8]
        else:
            out[pp, 4] = 1.0
    return np.ascontiguousarray(out.reshape(128, 6, NG, G).transpose(2, 0, 1, 3))


def na_rowmask(r0):
    out = np.full((8, 2, 64, 8, 64), NEG, np.float32)
    kc = np.arange(64)[:, None]
    qc = np.arange(64)[None, :]
    cs = np.clip(qc - 8, 0, 48)
    colok = (kc >= cs) & (kc < cs + 16)
    for j in range(8):
        for krl in range(2):
            kr = r0 - 4 + 2 * j + krl
            if kr < 0 or kr > 127:
                continue
            for qrl in range(8):
                qr = r0 + qrl
                rs = min(max(qr - 4, 0), 120)
                if rs <= kr < rs + 8:
                    out[j, krl, :, qrl, :] = np.where(colok, 0.0, NEG)
    return out.reshape(8, 128, 512)


def dil_bias():
    p = np.arange(128)[:, None]
    q = np.arange(512)[None, :]
    out = np.zeros((20, 128, 512), np.float32)
    for i in range(20):
        dl = 128 * i - 1024 + p - q
        a = np.abs(dl)
        c = (a <= 64).astype(np.int32) + ((dl % 4 == 0) & (a <= 256)) + ((dl % 16 == 0) & (a <= 1024))
        out[i] = np.where(c > 0, np.log(np.maximum(c, 1).astype(np.float32)), NEG)
    return out.astype(np.float32)


def block_consts():
    bd = np.zeros((128, 3, 128), np.float32)
    bd[:, 0, :] = 1.0
    for b in range(2):
        bd[b * 64:(b + 1) * 64, 1, b * 64:(b + 1) * 64] = 1.0
    for b in range(4):
        bd[b * 32:(b + 1) * 32, 2, b * 32:(b + 1) * 32] = 1.0
    return bd.reshape(128, 384)


class Buf:
    __slots__ = ("name", "t", "w", "r")

    def __init__(self, name, t=None):
        self.name = name
        self.t = t
        self.w = []
        self.r = []

    def __getitem__(self, idx):
        return self.t[idx]


class Trk:
    def __init__(self, nc, n_dma_sems=40):
        self.nc = nc
        self.eng = {"pe": nc.tensor, "act": nc.scalar, "dve": nc.vector, "pool": nc.gpsimd, "sp": nc.sync}
        self.sem, self.cnt = {}, {}
        for k in ("pe", "act", "dve", "pool"):
            self.sem[k] = nc.alloc_semaphore("s_" + k)
            self.cnt[k] = 0
        self.nd = n_dma_sems
        for i in range(n_dma_sems):
            self.sem["d%d" % i] = nc.alloc_semaphore("d%d" % i)
            self.cnt["d%d" % i] = 0
        self.dnext = 0
        self.known = {e: {} for e in self.eng}

    def _need(self, e, key, count):
        if key == e and e == "pe":
            return
        kn = self.known[e]
        if kn.get(key, 0) >= count:
            return
        self.eng[e].wait_ge(self.sem[key], count)
        kn[key] = count

    @staticmethod
    def _compress(lst):
        mx = {}
        for k_, c_ in lst:
            if mx.get(k_, 0) < c_:
                mx[k_] = c_
        return list(mx.items())

    def _deps(self, e, reads, writes, acc=False):
        for b in reads:
            for ww in b.w:
                self._need(e, *ww)
        if acc:
            return
        for b in writes:
            for ww in b.w:
                self._need(e, *ww)
            for rr in b.r:
                self._need(e, *rr)

    def _mark(self, ev, reads, writes, acc=False):
        for b in reads:
            b.r.append(ev)
            if len(b.r) > 48:
                b.r = self._compress(b.r)
        for b in writes:
            if acc:
                b.w.append(ev)
                if len(b.w) > 48:
                    b.w = self._compress(b.w)
            else:
                b.w = [ev]
                b.r = []

    def op(self, e, fn, reads=(), writes=()):
        self._deps(e, reads, writes)
        ins = fn(self.eng[e])
        self.cnt[e] += 1
        ins.then_inc(self.sem[e], 1)
        ev = (e, self.cnt[e])
        self._mark(ev, reads, writes)
        return ev

    def dma(self, e, out, in_, reads=(), writes=(), acc=False, **kw):
        self._deps(e, reads, writes, acc)
        key = "d%d" % self.dnext
        self.dnext = (self.dnext + 1) % self.nd
        if self.cnt[key] > 0:
            self._need(e, key, self.cnt[key])
        ins = self.eng[e].dma_start(out=out, in_=in_, **kw)
        self.cnt[key] += 16
        ins.then_inc(self.sem[key], 16)
        ev = (key, self.cnt[key])
        self._mark(ev, reads, writes, acc)
        return ev

    def wait_all(self, e, bufs):
        for b in bufs:
            for ww in b.w:
                self._need(e, *ww)
            for rr in b.r:
                self._need(e, *rr)


class Ring:
    def __init__(self, bufs):
        self.bufs = bufs
        self.i = 0

    def next(self):
        b = self.bufs[self.i]
        self.i = (self.i + 1) % len(self.bufs)
        return b


class Prog:
    def __init__(self, stages, exchange="host"):
        self.stages = stages
        self.exchange = exchange
        nc = self.nc = bass.Bass("TRN2", target_bir_lowering=False)
        self.k = Trk(nc)
        self.dram = {}
        self.ext_in, self.ext_out = [], []
        sset = set(stages)
        self.layers_w = sorted({l for (_, l) in stages if _ in ("TA", "TB")})

        def dt(name, shape, dtype, kind):
            t = nc.dram_tensor(name, shape, dtype, kind=kind)
            self.dram[name] = Buf(name, t.ap())
            if kind == "ExternalInput":
                self.ext_in.append(name)
            if kind == "ExternalOutput":
                self.ext_out.append(name)
            return self.dram[name]
        first = stages[0]
        last = stages[-1]
        if self.layers_w:
            self.x_in = dt("x_in", [8, 128, TOK], F32, "ExternalInput")
            self.x_out = dt("x_out", [8, 128, TOK], F32, "ExternalOutput")
        if self.layers_w:
            self.wall = dt("wall", [128, len(self.layers_w) * LAYER_W], F32, "ExternalInput")
            self.ws = dt("ws", [128, len(self.layers_w) * LAYER_W], BF16, "Internal")
        self.nsl = max(l_ for (_, l_) in stages) + 1
        self.vecs = dt("vecs", [128, self.nsl * NV], F32, "ExternalInput")
        self.bdc = dt("bdc", [128, 384], F32, "ExternalInput")
        if any(s_[0] == "TA" for s_ in stages):
            self.tabs = dt("tabs", [NG, 128, 6 * G], F32, "ExternalInput")
        if any(s_[0] == "ATT" for s_ in stages):
            self.lam = dt("lam", [64, 128 + 2], F32, "ExternalInput")
            self.natab = dt("natab", [4 * 23 * 64, 64], F32, "ExternalInput")
            self.rmc = dt("rmc", [24, 128, 512], F32, "ExternalInput")
            self.dilb = dt("dilb", [20, 128, 512], F32, "ExternalInput")
            self.vpn = dt("vpn", [128, 2], F32, "ExternalInput")
        self.qt, self.kto, self.vo, self.ktg, self.vg, self.mt = {}, {}, {}, {}, {}, {}
        for l in range(self.nsl):
            ta, att, tb = ("TA", l) in sset, ("ATT", l) in sset, ("TB", l) in sset
            if ta or att:
                kind_o = "Internal" if (ta and att) else ("ExternalOutput" if ta else "ExternalInput")
                self.qt[l] = dt("qt%d" % l, [8, 128, TOK], BF16, kind_o)
                if ta:
                    k2 = "Internal" if (att and exchange == "cc") else "ExternalOutput"
                    self.kto[l] = dt("kto%d" % l, [7 * 128, TOK], BF16, k2)
                    self.vo[l] = dt("vo%d" % l, [TOK, 896], BF16, k2)
                else:
                    self.kto[l] = dt("kto%d" % l, [7 * 128, TOK], BF16, "ExternalInput")
                    self.vo[l] = dt("vo%d" % l, [TOK, 896], BF16, "ExternalInput")
                if att:
                    k3 = "Internal" if (ta and exchange == "cc") else "ExternalInput"
                    self.ktg[l] = dt("ktg%d" % l, [2 * 7 * 128, TOK], BF16, k3)
                    self.vg[l] = dt("vg%d" % l, [2 * TOK, 896], BF16, k3)
            if att or tb:
                kind_m = "Internal" if (att and tb) else ("ExternalOutput" if att else "ExternalInput")
                self.mt[l] = dt("mt%d" % l, [8, 128, TOK], BF16, kind_m)
        self.build()

    def sb(self, name, shape, dtype):
        return Buf(name, self.nc.alloc_sbuf_tensor(name, shape, dtype))

    def build(self):
        nc, k = self.nc, self.k
        self.vec_sb = self.sb("vec_sb", [128, self.nsl * NV], F32)
        k.dma("sp", self.vec_sb[:], self.vecs[:, :], writes=[self.vec_sb])
        bdf = self.sb("bdf", [128, 384], F32)
        self.bd = self.sb("bd", [128, 384], BF16)
        k.dma("sp", bdf[:], self.bdc[:, :], writes=[bdf])
        k.op("dve", lambda e: e.tensor_copy(out=self.bd[:], in_=bdf[:]), reads=[bdf], writes=[self.bd])
        self.psum = Ring([Buf("ps%d" % i, nc.alloc_psum_tensor("ps%d" % i, [128, 512], F32)) for i in range(8)])
        self.cast_weights()
        i = 0
        st = self.stages
        while i < len(st):
            if st[i][0] in ("TA", "TB"):
                j = i
                while j < len(st) and st[j][0] in ("TA", "TB"):
                    j += 1
                self.t_phase(st[i:j])
                i = j
            else:
                self.att_phase(st[i][1])
                i += 1
        k.wait_all("sp", [self.dram[n] for n in self.ext_out])

    def cast_weights(self):
        nc, k = self.nc, self.k
        tot = len(self.layers_w) * LAYER_W
        if tot == 0:
            return
        CH = 4096
        with nc.sbuf_tensor("cst_f", [128, 3, CH], F32) as cf_t, nc.sbuf_tensor("cst_b", [128, 3, CH], BF16) as cb_t:
            cf = [Buf("cf%d" % i, cf_t[:, i, :]) for i in range(3)]
            cb = [Buf("cb%d" % i, cb_t[:, i, :]) for i in range(3)]
            engs = ["dve", "pool", "act"]
            i = 0
            for c0 in range(0, tot, CH):
                n = min(CH, tot - c0)
                f, b = cf[i % 3], cb[i % 3]
                k.dma("sp", f.t[:, 0:n], self.wall[:, c0:c0 + n], writes=[f])
                e = engs[i % 3]
                if e == "act":
                    k.op(e, lambda en, f=f, b=b, n=n: en.copy(out=b.t[:, 0:n], in_=f.t[:, 0:n]), reads=[f], writes=[b])
                else:
                    k.op(e, lambda en, f=f, b=b, n=n: en.tensor_copy(out=b.t[:, 0:n], in_=f.t[:, 0:n]), reads=[f], writes=[b])
                k.dma("pool", self.ws[:, c0:c0 + n], b.t[:, 0:n], reads=[b], writes=[self.ws], acc=True)
                i += 1
            k.wait_all("sp", [self.ws])
            k.wait_all("pool", cf + cb)
            k.wait_all("dve", cf + cb)
            k.wait_all("act", cf + cb)
            k.wait_all("sp", cf + cb)

    def t_phase(self, tst):
        nc, k = self.nc, self.k
        vec = self.vec_sb
        with (nc.sbuf_tensor("t_x", [128, 2, 8, G], F32) as x_t,
              nc.sbuf_tensor("t_m", [128, 2, 8, G], BF16) as m_t,
              nc.sbuf_tensor("t_h", [128, 8, G], BF16) as h_t,
              nc.sbuf_tensor("t_act", [128, NFC, G], BF16) as act_t,
              nc.sbuf_tensor("t_sq", [128, 3, G], BF16) as sq_t,
              nc.sbuf_tensor("t_f32", [128, 8, G], F32) as f32_t,
              nc.sbuf_tensor("t_qo", [128, 3, G], BF16) as qo_t,
              nc.sbuf_tensor("t_vo", [128, 2, 896], BF16) as vo_t,
              nc.sbuf_tensor("t_tab", [128, 6, G], F32) as tab_t,
              nc.sbuf_tensor("t_w", [128, NSLOT, SLOT], BF16) as w_t):
            xb = [[Buf("x%d_%d" % (i, c), x_t[:, i, c, :]) for c in range(8)] for i in range(2)]
            mbb = [[Buf("m%d_%d" % (i, c), m_t[:, i, c, :]) for c in range(8)] for i in range(2)]
            hb = [Buf("h%d" % c, h_t[:, c, :]) for c in range(8)]
            actb = [Buf("act%d" % i, act_t[:, i, :]) for i in range(NFC)]
            sqr = Ring([Buf("sq%d" % i, sq_t[:, i, :]) for i in range(3)])
            f32r = Ring([Buf("f32_%d" % i, f32_t[:, i, :]) for i in range(8)])
            qor = Ring([Buf("qo%d" % i, qo_t[:, i, :]) for i in range(3)])
            vor = Ring([Buf("vo%d" % i, vo_t[:, i, :]) for i in range(2)])
            tabb = Buf("tab", tab_t[:])
            wslots = Ring([Buf("w%d" % i, w_t[:, i, :]) for i in range(NSLOT)])

            def stage_tiles(kind, l):
                lp = self.layers_w.index(l)
                seq = []
                if kind == "TB":
                    seq += [(lp, "wo", i) for i in range(4)]
                    seq += [(lp, "f2gu", i) for i in range(NFC)] + [(lp, "f2d", i) for i in range(8)]
                else:
                    seq += [(lp, "f1gu", i) for i in range(NFC)] + [(lp, "f1d", i) for i in range(8)]
                    seq += [(lp, "qk", i) for i in range(len(QK_CHUNKS))] + [(lp, "v", i) for i in range(2)]
                return seq
            wseq = []
            for g in range(NG):
                for (kind, l) in tst:
                    wseq += stage_tiles(kind, l)
            wstate = {"issued": 0, "taken": 0, "q": []}

            def w_issue():
                lp, name, idx = wseq[wstate["issued"]]
                o, n = TILE_TAB[(name, idx)]
                slot = wslots.next()
                k.dma("sp", slot.t[:, 0:n], self.ws[:, lp * LAYER_W + o:lp * LAYER_W + o + n],
                      reads=[self.ws], writes=[slot])
                wstate["q"].append((slot, name, idx))
                wstate["issued"] += 1

            def w_get(name, idx):
                while wstate["issued"] < len(wseq) and wstate["issued"] - wstate["taken"] < NSLOT - 1:
                    w_issue()
                slot, nm, ix = wstate["q"].pop(0)
                assert (nm, ix) == (name, idx), (nm, ix, name, idx)
                wstate["taken"] += 1
                return slot

            has_tb = tst[0][0] == "TB"
            l_tb = tst[0][1] if has_tb else None
            ta_list = [s for s in tst if s[0] == "TA"]
            l_ta = ta_list[0][1] if ta_list else None
            if has_tb:
                x_src = self.dram["x_in"] if ("TA", l_tb) not in set(self.stages) else self.x1
            else:
                x_src = self.dram["x_in"]
            if l_ta is not None:
                x_dst = self.dram["x_out"] if ("TB", l_ta) not in set(self.stages) else self._x1_tensor()
            else:
                x_dst = self.dram["x_out"]

            def load_group(g):
                xg = xb[g % 2]
                k.dma("pool", x_t[:, g % 2], x_src[:, :, g * G:(g + 1) * G].rearrange("c p t -> p c t"),
                      reads=[x_src], writes=xg)
                if has_tb:
                    k.dma("pool", m_t[:, g % 2], self.mt[l_tb][:, :, g * G:(g + 1) * G].rearrange("c p t -> p c t"),
                          reads=[self.mt[l_tb]], writes=mbb[g % 2])

            def norm(xg, gcol0):
                ss = self.psum.next()
                for c in range(8):
                    sq = sqr.next()
                    k.op("pool", lambda e, sq=sq, c=c: e.tensor_tensor(out=sq.t, in0=xg[c].t, in1=xg[c].t, op=ALU.mult),
                         reads=[xg[c]], writes=[sq])
                    k.op("pe", lambda e, sq=sq, c=c: e.matmul(ss.t[:, :], lhsT=self.bd[:, 0:128], rhs=sq.t, start=(c == 0), stop=(c == 7)),
                         reads=[sq, self.bd], writes=[ss])
                rs = f32r.next()
                k.op("act", lambda e: e.activation(out=rs.t, in_=ss.t[:, :], func=AF.Sqrt, scale=1.0 / D, bias=EPS),
                     reads=[ss], writes=[rs])
                k.op("dve", lambda e: e.reciprocal(out=rs.t, in_=rs.t), reads=[rs], writes=[rs])
                for c in range(8):
                    k.op("dve", lambda e, c=c: e.scalar_tensor_tensor(out=hb[c].t, in0=xg[c].t,
                                                                      scalar=vec[:, gcol0 + c:gcol0 + c + 1], in1=rs.t,
                                                                      op0=ALU.mult, op1=ALU.mult),
                         reads=[xg[c], rs, vec], writes=[hb[c]])

            def ffn(xg, ff):
                for fc in range(NFC):
                    w = w_get(ff + "gu", fc)
                    wv_ = w.t[:, 0:2048].rearrange("p (a k f) -> p a k f", a=2, k=8)
                    gp, up = self.psum.next(), self.psum.next()
                    for a, pt in ((0, gp), (1, up)):
                        for kk in range(8):
                            k.op("pe", lambda e, a=a, pt=pt, kk=kk: e.matmul(pt.t[:, :], lhsT=wv_[:, a, kk, :], rhs=hb[kk].t,
                                                                            start=(kk == 0), stop=(kk == 7)),
                                 reads=[w, hb[kk]], writes=[pt])
                    sg = f32r.next()
                    k.op("act", lambda e: e.activation(out=sg.t, in_=gp.t[:, :], func=AF.Silu), reads=[gp], writes=[sg])
                    k.op("dve", lambda e, fc=fc: e.tensor_tensor(out=actb[fc].t, in0=sg.t, in1=up.t[:, :], op=ALU.mult),
                         reads=[sg, up], writes=[actb[fc]])
                for c in range(8):
                    w = w_get(ff + "d", c)
                    wv_ = w.t[:, 0:2816].rearrange("p (k f) -> p k f", k=NFC)
                    yp = self.psum.next()
                    for fk in range(NFC):
                        k.op("pe", lambda e, fk=fk: e.matmul(yp.t[:, :], lhsT=wv_[:, fk, :], rhs=actb[fk].t,
                                                             start=(fk == 0), stop=(fk == NFC - 1)),
                             reads=[w, actb[fk]], writes=[yp])
                    k.op("dve", lambda e, c=c: e.scalar_tensor_tensor(out=xg[c].t, in0=yp.t[:, :], scalar=0.5,
                                                                      in1=xg[c].t, op0=ALU.mult, op1=ALU.add),
                         reads=[yp, xg[c]], writes=[xg[c]])

            def out_proj(xg, mb):
                for i in range(4):
                    w = w_get("wo", i)
                    wv_ = w.t[:, 0:2048].rearrange("p (a k f) -> p a k f", a=2, k=8)
                    for a in range(2):
                        c = 2 * i + a
                        yp = self.psum.next()
                        for kk in range(8):
                            k.op("pe", lambda e, a=a, kk=kk: e.matmul(yp.t[:, :], lhsT=wv_[:, a, kk, :], rhs=mb[kk].t,
                                                                      start=(kk == 0), stop=(kk == 7)),
                                 reads=[w, mb[kk]], writes=[yp])
                        k.op("dve", lambda e, c=c: e.tensor_tensor(out=xg[c].t, in0=yp.t[:, :], in1=xg[c].t, op=ALU.add),
                             reads=[yp, xg[c]], writes=[xg[c]])

            def in_proj(g, l):
                vb = l * NV
                k.dma("pool", tabb.t, self.tabs[g].rearrange("p (a t) -> p a t", a=6), reads=[self.tabs], writes=[tabb])
                qt_d, kt_d, v_d = self.qt[l], self.kto[l], self.vo[l]
                for ci, (_, typ, is_k) in enumerate(QK_CHUNKS):
                    w = w_get("qk", ci)
                    roped = typ != "na"
                    d = HD[typ]
                    if roped:
                        wv_ = w.t[:, 0:2048].rearrange("p (a k f) -> p a k f", a=2, k=8)
                    else:
                        wv_ = w.t[:, 0:1024].rearrange("p (a k f) -> p a k f", a=1, k=8)
                    pm = self.psum.next()
                    psw = self.psum.next() if roped else None
                    for a, pt in ((0, pm), (1, psw)):
                        if pt is None:
                            continue
                        for kk in range(8):
                            k.op("pe", lambda e, a=a, pt=pt, kk=kk: e.matmul(pt.t[:, :], lhsT=wv_[:, a, kk, :], rhs=hb[kk].t,
                                                                            start=(kk == 0), stop=(kk == 7)),
                                 reads=[w, hb[kk]], writes=[pt])
                    sq = sqr.next()
                    k.op("act", lambda e: e.activation(out=sq.t, in_=pm.t[:, :], func=AF.Square), reads=[pm], writes=[sq])
                    ss = self.psum.next()
                    bsel = 1 if d == 64 else 2
                    k.op("pe", lambda e: e.matmul(ss.t[:, :], lhsT=self.bd[:, bsel * 128:(bsel + 1) * 128], rhs=sq.t, start=True, stop=True),
                         reads=[sq, self.bd], writes=[ss])
                    rs = f32r.next()
                    k.op("act", lambda e: e.activation(out=rs.t, in_=ss.t[:, :], func=AF.Sqrt, scale=1.0 / d, bias=EPS),
                         reads=[ss], writes=[rs])
                    k.op("dve", lambda e: e.reciprocal(out=rs.t, in_=rs.t), reads=[rs], writes=[rs])
                    gm, gs = GCOL[(typ, is_k)]
                    qo = qor.next()
                    if not roped:
                        k.op("dve", lambda e: e.scalar_tensor_tensor(out=qo.t, in0=pm.t[:, :], scalar=vec[:, vb + gm:vb + gm + 1],
                                                                     in1=rs.t, op0=ALU.mult, op1=ALU.mult),
                             reads=[pm, rs, vec], writes=[qo])
                    else:
                        ti = TABI[typ]
                        t1, t2 = f32r.next(), f32r.next()
                        k.op("dve", lambda e: e.scalar_tensor_tensor(out=t1.t, in0=pm.t[:, :], scalar=vec[:, vb + gm:vb + gm + 1],
                                                                     in1=tabb.t[:, ti, :], op0=ALU.mult, op1=ALU.mult),
                             reads=[pm, tabb, vec], writes=[t1])
                        k.op("dve", lambda e: e.scalar_tensor_tensor(out=t2.t, in0=psw.t[:, :], scalar=vec[:, vb + gs:vb + gs + 1],
                                                                     in1=tabb.t[:, ti + 1, :], op0=ALU.mult, op1=ALU.mult),
                             reads=[psw, tabb, vec], writes=[t2])
                        k.op("pool", lambda e: e.tensor_tensor(out=t1.t, in0=t1.t, in1=t2.t, op=ALU.add), reads=[t1, t2], writes=[t1])
                        k.op("pool", lambda e: e.tensor_tensor(out=qo.t, in0=t1.t, in1=rs.t, op=ALU.mult), reads=[t1, rs], writes=[qo])
                    if not is_k:
                        k.dma("pool", qt_d[ci, :, g * G:(g + 1) * G], qo.t, reads=[qo], writes=[qt_d], acc=True)
                    else:
                        r0 = (ci - 8) * 128
                        k.dma("pool", kt_d[r0:r0 + 128, g * G:(g + 1) * G], qo.t, reads=[qo], writes=[kt_d], acc=True)
                wv0, wv1 = w_get("v", 0), w_get("v", 1)
                wvs = [wv0.t[:, 0:3584].rearrange("p (k f) -> p k f", k=8), wv1.t[:, 0:3584].rearrange("p (k f) -> p k f", k=8)]
                for tt in range(4):
                    vo_b = vor.next()
                    for hf in range(2):
                        vp_ = self.psum.next()
                        for kk in range(8):
                            k.op("pe", lambda e, hf=hf, kk=kk, tt=tt: e.matmul(vp_.t[:, 0:448], lhsT=hb[kk].t[:, tt * 128:(tt + 1) * 128],
                                                                               rhs=wvs[hf][:, kk, :], start=(kk == 0), stop=(kk == 7)),
                                 reads=[(wv0, wv1)[hf], hb[kk]], writes=[vp_])
                        k.op("act", lambda e, hf=hf: e.copy(out=vo_b.t[:, hf * 448:(hf + 1) * 448], in_=vp_.t[:, 0:448]),
                             reads=[vp_], writes=[vo_b])
                    t0 = g * G + tt * 128
                    k.dma("pool", v_d[t0:t0 + 128, :], vo_b.t, reads=[vo_b], writes=[v_d], acc=True)

            load_group(0)
            for g in range(NG):
                xg = xb[g % 2]
                if g + 1 < NG:
                    load_group(g + 1)
                for (kind, l) in tst:
                    vb = l * NV
                    if kind == "TB":
                        out_proj(xg, mbb[g % 2])
                        norm(xg, vb + 16)
                        ffn(xg, "f2")
                    else:
                        norm(xg, vb + 0)
                        ffn(xg, "f1")
                        norm(xg, vb + 8)
                        in_proj(g, l)
                k.dma("pool", x_dst[:, :, g * G:(g + 1) * G].rearrange("c p t -> p c t"), x_t[:, g % 2], reads=xg, writes=[x_dst])
            allb = xb[0] + xb[1] + mbb[0] + mbb[1] + hb + actb + sqr.bufs + f32r.bufs + qor.bufs + vor.bufs + [tabb] + wslots.bufs
            for e in ("sp", "pool", "act", "dve", "pe"):
                k.wait_all(e, allb)

    def _x1_tensor(self):
        if not hasattr(self, "x1"):
            t = self.nc.dram_tensor("x1s", [8, 128, TOK], F32, kind="Internal")
            self.x1 = Buf("x1s", t.ap())
        return self.x1

    def att_phase(self, l):
        nc, k = self.nc, self.k
        vec = self.vec_sb
        vb = l * NV
        ktg, vg, kto, vo, qt, mt = self.ktg[l], self.vg[l], self.kto[l], self.vo[l], self.qt[l], self.mt[l]
        ps = self.psum.bufs
        s_ring, o_ring, m_ring = Ring(ps[0:4]), Ring(ps[4:6]), Ring(ps[6:8])
        with (nc.sbuf_tensor("a_kt", [64, 8192], BF16) as kt_t,
              nc.sbuf_tensor("a_va", [128, 64, 128], BF16) as va_t,
              nc.sbuf_tensor("a_q", [64, TOK], BF16) as q_t,
              nc.sbuf_tensor("a_p", [128, 6, 512], BF16) as p_t,
              nc.sbuf_tensor("a_s", [128, 6, 512], F32) as s_t,
              nc.sbuf_tensor("a_bias", [128, 32, 512], F32) as b_t,
              nc.sbuf_tensor("a_post", [128, 8, 512], F32) as post_t,
              nc.sbuf_tensor("a_ob", [64, 3, 512], BF16) as ob_t,
              nc.sbuf_tensor("a_small", [64, 160], F32) as sm_t,
              nc.sbuf_tensor("a_vpn", [128, 2], F32) as vpn_t):
            ktb = Buf("kt", kt_t[:])
            vab = Buf("va", va_t[:])
            vones = Buf("vones", None)
            qb = Buf("q", q_t[:])
            pr = Ring([Buf("p%d" % i, p_t[:, i, :]) for i in range(6)])
            sr = Ring([Buf("s%d" % i, s_t[:, i, :]) for i in range(6)])
            bias = [Buf("b%d" % i, b_t[:, i, :]) for i in range(32)]
            postr = Ring([Buf("po%d" % i, post_t[:, i, :]) for i in range(8)])
            obr = Ring([Buf("ob%d" % i, ob_t[:, i, :]) for i in range(3)])
            smb = Buf("sm", sm_t[:])
            vpnb = Buf("vpn", vpn_t[:])
            k.dma("sp", vpn_t[:], self.vpn[:, :], writes=[vpnb])
            k.dma("sp", sm_t[:, 0:130], self.lam[:, :], writes=[smb])
            k.op("dve", lambda e: e.tensor_tensor(out=sm_t[:, 0:32], in0=sm_t[:, 0:32], in1=sm_t[:, 32:64], op=ALU.mult), reads=[smb], writes=[smb])
            k.op("dve", lambda e: e.tensor_tensor(out=sm_t[:, 64:96], in0=sm_t[:, 64:96], in1=sm_t[:, 96:128], op=ALU.mult), reads=[smb], writes=[smb])
            k.op("dve", lambda e: e.reduce_sum(out=sm_t[:, 130:131], in_=sm_t[:, 0:32], axis=mybir.AxisListType.X), reads=[smb], writes=[smb])
            k.op("dve", lambda e: e.reduce_sum(out=sm_t[:, 131:132], in_=sm_t[:, 64:96], axis=mybir.AxisListType.X), reads=[smb], writes=[smb])
            k.op("act", lambda e: e.activation(out=sm_t[:, 132:134], in_=sm_t[:, 130:132], func=AF.Exp), reads=[smb], writes=[smb])
            k.op("dve", lambda e: e.tensor_tensor(out=sm_t[:, 134:135], in0=sm_t[:, 133:134], in1=sm_t[:, 132:133], op=ALU.subtract), reads=[smb], writes=[smb])
            k.op("dve", lambda e: e.tensor_tensor(out=sm_t[:, 134:135], in0=sm_t[:, 134:135], in1=sm_t[:, 128:129], op=ALU.add), reads=[smb], writes=[smb])
            k.op("dve", lambda e: e.tensor_tensor(out=sm_t[:, 135:136], in0=vec[0:64, vb + 38:vb + 39], in1=sm_t[:, 129:130], op=ALU.mult), reads=[smb, vec], writes=[smb])
            k.op("pool", lambda e: e.memset(va_t[:, :, 64:128], 1.0), writes=[vones])
            k.op("pool", lambda e: e.tensor_scalar(out=va_t[:, 0:8, 64:128], in0=va_t[:, 0:8, 64:128], scalar1=vpn_t[:, 0:1], scalar2=1.0, op0=ALU.mult, op1=ALU.mult),
                 reads=[vpnb, vones], writes=[vones])
            k.op("pool", lambda e: e.tensor_scalar(out=va_t[:, 40:48, 64:128], in0=va_t[:, 40:48, 64:128], scalar1=vpn_t[:, 1:2], scalar2=1.0, op0=ALU.mult, op1=ALU.mult),
                 reads=[vpnb, vones], writes=[vones])

            def load_head(krow, qrow, ext):
                qc, qp = qrow // 128, qrow % 128
                k.dma("sp", q_t[:], qt[qc, qp:qp + 64, :], reads=[qt], writes=[qb])
                if ext:
                    k.dma("sp", kt_t[:, 0:1024], ktg[krow:krow + 64, 3072:4096], reads=[ktg], writes=[ktb])
                    k.dma("sp", kt_t[:, 1024:5120], kto[krow:krow + 64, :], reads=[kto], writes=[ktb], acc=True)
                    k.dma("sp", kt_t[:, 5120:6144], ktg[896 + krow:896 + krow + 64, 0:1024], reads=[ktg], writes=[ktb], acc=True)
                    k.dma("sp", va_t[:, 0:8, 0:64], vg[3072:4096, krow:krow + 64].rearrange("(c p) d -> p c d", p=128),
                          reads=[vg], writes=[vab])
                    k.dma("sp", va_t[:, 8:40, 0:64], vo[:, krow:krow + 64].rearrange("(c p) d -> p c d", p=128),
                          reads=[vo], writes=[vab], acc=True)
                    k.dma("sp", va_t[:, 40:48, 0:64], vg[TOK:TOK + 1024, krow:krow + 64].rearrange("(c p) d -> p c d", p=128),
                          reads=[vg], writes=[vab], acc=True)
                    k.op("pool", lambda e: e.tensor_scalar(out=va_t[:, 0:8, 0:64], in0=va_t[:, 0:8, 0:64], scalar1=vpn_t[:, 0:1], scalar2=1.0, op0=ALU.mult, op1=ALU.mult),
                         reads=[vab, vpnb], writes=[vab])
                    k.op("pool", lambda e: e.tensor_scalar(out=va_t[:, 40:48, 0:64], in0=va_t[:, 40:48, 0:64], scalar1=vpn_t[:, 1:2], scalar2=1.0, op0=ALU.mult, op1=ALU.mult),
                         reads=[vab, vpnb], writes=[vab])
                else:
                    for r in range(2):
                        k.dma("sp", kt_t[:, r * TOK:(r + 1) * TOK], ktg[r * 896 + krow:r * 896 + krow + 64, :], reads=[ktg], writes=[ktb], acc=(r == 1))
                        k.dma("sp", va_t[:, r * 32:(r + 1) * 32, 0:64], vg[r * TOK:(r + 1) * TOK, krow:krow + 64].rearrange("(c p) d -> p c d", p=128),
                              reads=[vg], writes=[vab], acc=(r == 1))

            def attend(qg, chunks, scale, p0, kd, bias_fn=None, rm_fn=None):
                O = o_ring.next()
                n = len(chunks)
                pbufs = [None] * n
                for i in range(n + 2):
                    if i < n:
                        kc = chunks[i]
                        sp_ = s_ring.next()
                        k.op("pe", lambda e, kc=kc, sp_=sp_: e.matmul(sp_.t[:, :], lhsT=kt_t[p0:p0 + kd, kc * 128:(kc + 1) * 128],
                                                                     rhs=q_t[p0:p0 + kd, qg * G:(qg + 1) * G], start=True, stop=True),
                             reads=[ktb, qb], writes=[sp_])
                        pb = pr.next()
                        if bias_fn is None:
                            k.op("act", lambda e, sp_=sp_, pb=pb: e.activation(out=pb.t, in_=sp_.t[:, :], func=AF.Exp, scale=scale),
                                 reads=[sp_], writes=[pb])
                        else:
                            bb = bias_fn(i)
                            s2 = sr.next()
                            k.op("dve", lambda e, sp_=sp_, s2=s2, bb=bb: e.scalar_tensor_tensor(out=s2.t, in0=sp_.t[:, :], scalar=scale, in1=bb.t,
                                                                                               op0=ALU.mult, op1=ALU.add),
                                 reads=[sp_, bb], writes=[s2])
                            if rm_fn is not None:
                                rb_ = rm_fn(i)
                                k.op("pool", lambda e, s2=s2, rb_=rb_: e.tensor_tensor(out=s2.t, in0=s2.t, in1=rb_.t, op=ALU.add),
                                     reads=[s2, rb_], writes=[s2])
                            k.op("act", lambda e, s2=s2, pb=pb: e.activation(out=pb.t, in_=s2.t, func=AF.Exp), reads=[s2], writes=[pb])
                        pbufs[i] = pb
                    j = i - 2
                    if j >= 0:
                        kc = chunks[j]
                        k.op("pe", lambda e, kc=kc, j=j: e.matmul(O.t[:, :], lhsT=va_t[:, kc, :], rhs=pbufs[j].t, start=(j == 0), stop=(j == n - 1)),
                             reads=[vab, vones, pbufs[j]], writes=[O])
                return O

            def normalize(O, dst_ap, dst_buf):
                rsb = postr.next()
                k.op("act", lambda e: e.copy(out=rsb.t[64:128, :], in_=O.t[64:128, :]), reads=[O], writes=[rsb])
                k.op("dve", lambda e: e.reciprocal(out=rsb.t[64:128, :], in_=rsb.t[64:128, :]), reads=[rsb], writes=[rsb])
                k.op("dve", lambda e: e.tensor_tensor(out=dst_ap, in0=O.t[0:64, :], in1=rsb.t[64:128, :], op=ALU.mult),
                     reads=[O, rsb], writes=[dst_buf])

            def store(ob, orow, qg):
                oc, op_ = orow // 128, orow % 128
                k.dma("pool", mt[oc, op_:op_ + 64, qg * G:(qg + 1) * G], ob.t, reads=[ob], writes=[mt], acc=True)

            def simple_head(orow, chunks_fn, scale, bias_fn=None, rm_fn=None):
                for qg in range(NG):
                    O = attend(qg, chunks_fn(qg), scale, 0, 64,
                               (lambda i, qg=qg: bias_fn(qg, i)) if bias_fn else None,
                               (lambda i, qg=qg: rm_fn(qg, i)) if rm_fn else None)
                    ob = obr.next()
                    normalize(O, ob.t, ob)
                    store(ob, orow, qg)

            k.dma("sp", b_t[:, 8:32, :], self.rmc[:, :, :].rearrange("n p t -> p n t"), writes=bias[8:32])
            for h in range(4):
                load_head(64 * h, 64 * h, True)
                for j in range(8):
                    for krl in range(2):
                        a0 = 15 - 2 * j - krl
                        src = self.natab[(h * 23 + a0) * 64:(h * 23 + a0 + 8) * 64, :].rearrange("(a kc) qc -> kc a qc", kc=64)
                        k.dma("sp", b_t[krl * 64:(krl + 1) * 64, j, :].rearrange("p (a q) -> p a q", a=8), src,
                              reads=[self.natab], writes=[bias[j]], acc=(krl == 1))
                simple_head(64 * h, lambda qg: [6 + 4 * qg + j for j in range(8)], 0.125,
                            bias_fn=lambda qg, i: bias[i],
                            rm_fn=lambda qg, i: bias[8 + (0 if qg == 0 else (2 if qg == NG - 1 else 1)) * 8 + i])
            k.dma("sp", b_t[:, 0:20, :], self.dilb[:, :, :].rearrange("n p t -> p n t"), writes=bias[0:20])
            for h in range(4):
                load_head(640 + 64 * h, 768 + 64 * h, True)
                simple_head(768 + 64 * h, lambda qg: [4 * qg + i for i in range(20)], 0.125, bias_fn=lambda qg, i: bias[i])
            k.op("pool", lambda e: e.memset(va_t[:, :, 64:128], 1.0), reads=[vab], writes=[vones])
            allc = list(range(64))
            for qh in range(4):
                load_head(512 + 64 * (qh // 2), 512 + 64 * qh, False)
                simple_head(512 + 64 * qh, lambda qg: allc, 0.125)
            dsc = 32 ** -0.5
            for h in range(4):
                load_head(256 + 64 * h, 256 + 64 * h, False)
                for qg in range(NG):
                    o12 = []
                    for mp in range(2):
                        O = attend(qg, allc, dsc, 32 * mp, 32)
                        ob_ = postr.next()
                        normalize(O, ob_.t[0:64, :], ob_)
                        o12.append(ob_)
                    od = postr.next()
                    k.op("dve", lambda e: e.scalar_tensor_tensor(out=od.t[0:64, :], in0=o12[1].t[0:64, :], scalar=sm_t[:, 134:135], in1=o12[0].t[0:64, :],
                                                                 op0=ALU.mult, op1=ALU.add), reads=[o12[0], o12[1], smb], writes=[od])
                    sq = pr.next()
                    k.op("pool", lambda e: e.tensor_tensor(out=sq.t[0:64, :], in0=od.t[0:64, :], in1=od.t[0:64, :], op=ALU.mult), reads=[od], writes=[sq])
                    ss = m_ring.next()
                    k.op("pe", lambda e: e.matmul(ss.t[0:64, :], lhsT=self.bd[0:64, 128:192], rhs=sq.t[0:64, :], start=True, stop=True),
                         reads=[sq, self.bd], writes=[ss])
                    ln = postr.next()
                    k.op("act", lambda e: e.activation(out=ln.t[0:64, :], in_=ss.t[0:64, :], func=AF.Ln, scale=1.0 / 64, bias=EPS), reads=[ss], writes=[ln])
                    k.op("act", lambda e: e.activation(out=ln.t[0:64, :], in_=ln.t[0:64, :], func=AF.Exp, scale=-0.5), reads=[ln], writes=[ln])
                    ob = obr.next()
                    k.op("dve", lambda e: e.scalar_tensor_tensor(out=ob.t, in0=od.t[0:64, :], scalar=sm_t[:, 135:136], in1=ln.t[0:64, :],
                                                                 op0=ALU.mult, op1=ALU.mult), reads=[od, ln, smb], writes=[ob])
                    store(ob, 256 + 64 * h, qg)
            allb = [ktb, vab, vones, qb, smb, vpnb] + pr.bufs + sr.bufs + bias + postr.bufs + obr.bufs
            for e in ("sp", "pool", "act", "dve", "pe"):
                k.wait_all(e, allb)


def _core_inputs(inp):
    wall = prep_weights(inp)
    vecs, lam, natab = prep_vecs(inp)
    dilb = dil_bias()
    bdc = block_consts()
    rm_first, rm_int, rm_last = na_rowmask(0), na_rowmask(16), na_rowmask(120)
    per_core = []
    for c in range(8):
        b, half = c // 2, c % 2
        xT = np.ascontiguousarray(inp["x"][b, half * TOK:(half + 1) * TOK, :].T).reshape(8, 128, TOK)
        rmc = np.stack([rm_first if half == 0 else rm_int, rm_int, rm_last if half == 1 else rm_int]).reshape(24, 128, 512)
        vpn = np.zeros((128, 2), np.float32)
        vpn[:, 0] = 1.0 if half == 1 else 0.0
        vpn[:, 1] = 1.0 if half == 0 else 0.0
        per_core.append({"x_in": xT, "vecs": vecs, "lam": lam, "natab": natab.reshape(-1, 64),
                         "tabs": rope_tables_core(half).reshape(NG, 128, 6 * G), "rmc": rmc, "dilb": dilb,
                         "bdc": bdc, "vpn": vpn})
    return wall, per_core


def _run(prog, maps):
    res = run_bass_kernel_spmd(prog.nc, maps, core_ids=list(range(8)))
    return [r for r in res.results]


def kernel(**inputs):
    inp = {k_: np.asarray(v) for k_, v in inputs.items()}
    wall, per_core = _core_inputs(inp)
    vecs_all = per_core[0]["vecs"]
    lam_all = per_core[0]["lam"]
    natab_all = per_core[0]["natab"].reshape(L, 4 * 23 * 64, 64)

    def wslice(ls):
        return np.ascontiguousarray(np.concatenate([wall[:, l_ * LAYER_W:(l_ + 1) * LAYER_W] for l_ in ls], axis=1))

    def vslice(ls):
        return np.ascontiguousarray(np.concatenate([vecs_all[:, l_ * NV:(l_ + 1) * NV] for l_ in ls], axis=1))

    pa = Prog([("TA", 0)])
    wa, va = wslice([0]), vslice([0])
    maps = [{"x_in": per_core[c]["x_in"], "wall": wa, "vecs": va, "bdc": per_core[c]["bdc"], "tabs": per_core[c]["tabs"]}
            for c in range(8)]
    res = _run(pa, maps)
    cur = [{"x": r["x_out"], "qt": r["qt0"], "kto": r["kto0"], "vo": r["vo0"]} for r in res]
    for l in range(L):
        last = l == L - 1
        stages = [("ATT", 0), ("TB", 0)] + ([] if last else [("TA", 1)])
        pb = Prog(stages)
        ls = [l] if last else [l, l + 1]
        wb, vb_ = wslice(ls), vslice(ls)
        lam_init = 0.8 - 0.6 * math.exp(-0.3 * l)
        laml = np.ascontiguousarray(np.concatenate(
            [lam_all[:, l * 128:(l + 1) * 128], np.full((64, 1), -lam_init, np.float32),
             np.full((64, 1), 1.0 - lam_init, np.float32)], axis=1))
        maps = []
        for c in range(8):
            pr = c - (c % 2)
            m = {"x_in": cur[c]["x"], "wall": wb, "vecs": vb_, "bdc": per_core[c]["bdc"], "lam": laml,
                 "natab": np.ascontiguousarray(natab_all[l]), "rmc": per_core[c]["rmc"], "dilb": per_core[c]["dilb"],
                 "vpn": per_core[c]["vpn"], "qt0": cur[c]["qt"], "kto0": cur[c]["kto"], "vo0": cur[c]["vo"],
                 "ktg0": np.concatenate([cur[pr]["kto"], cur[pr + 1]["kto"]], axis=0),
                 "vg0": np.concatenate([cur[pr]["vo"], cur[pr + 1]["vo"]], axis=0)}
            if not last:
                m["tabs"] = per_core[c]["tabs"]
            maps.append(m)
        res = _run(pb, maps)
        if last:
            cur = [{"x": r["x_out"]} for r in res]
        else:
            cur = [{"x": r["x_out"], "qt": r["qt1"], "kto": r["kto1"], "vo": r["vo1"]} for r in res]
    out = np.empty((4, S, D), np.float32)
    for c in range(8):
        b, half = c // 2, c % 2
        out[b, half * TOK:(half + 1) * TOK, :] = np.asarray(cur[c]["x"], np.float32).reshape(D, TOK).T
    return out
```

```python
import math
import numpy as np
import concourse.bass as bass
import concourse.mybir as mybir
from concourse.bass_utils import run_bass_kernel_spmd

F32 = mybir.dt.float32
BF16 = mybir.dt.bfloat16
AF = mybir.ActivationFunctionType
ALU = mybir.AluOpType

D = 1024
F = 2816
NFC = 22
L = 4
S = 8192
TOK = 4096
G = 512
NG = TOK // G
EPS = 1e-6
NEG = -30000.0
NV = 40
SLOT = 3584
NSLOT = 6

C_NAQ, C_NAK, C_NAV, C_DFQ, C_DFK, C_DFV, C_GQQ, C_GQK, C_GQV, C_DLQ, C_DLK, C_DLV = (
    0, 256, 512, 768, 1024, 1280, 1536, 1792, 1920, 2048, 2304, 2560)
QK_CHUNKS = [(C_NAQ, "na", 0), (C_NAQ + 128, "na", 0), (C_DFQ, "diff", 0), (C_DFQ + 128, "diff", 0),
             (C_GQQ, "gqa", 0), (C_GQQ + 128, "gqa", 0), (C_DLQ, "dil", 0), (C_DLQ + 128, "dil", 0),
             (C_NAK, "na", 1), (C_NAK + 128, "na", 1), (C_DFK, "diff", 1), (C_DFK + 128, "diff", 1),
             (C_GQK, "gqa", 1), (C_DLK, "dil", 1), (C_DLK + 128, "dil", 1)]
HD = {"na": 64, "diff": 32, "gqa": 64, "dil": 64}
GCOL = {("na", 0): (24, None), ("na", 1): (25, None), ("diff", 0): (26, 27), ("diff", 1): (28, 29),
        ("gqa", 0): (30, 31), ("gqa", 1): (32, 33), ("dil", 0): (34, 35), ("dil", 1): (36, 37)}
TABI = {"diff": 0, "gqa": 2, "dil": 4}
V_COLS = [(C_NAV, 256), (C_DFV, 256), (C_GQV, 128), (C_DLV, 256)]


def swap_index(typ):
    d = HD[typ]
    idx = np.arange(d)
    if typ == "gqa":
        jj = idx % 32
        return (idx // 32) * 32 + (jj + 16) % 32
    if typ == "diff":
        out = idx.copy()
        out[:8] = (idx[:8] + 4) % 8
        return out
    if typ == "dil":
        out = idx.copy()
        out[:16] = (idx[:16] + 8) % 16
        return out
    return idx


def layer_tile_table():
    tab = {}
    off = 0

    def add(name, i, n):
        nonlocal off
        tab[(name, i)] = (off, n)
        off += n
    for ff in ("f1", "f2"):
        for fc in range(NFC):
            add(ff + "gu", fc, 2048)
        for c in range(8):
            add(ff + "d", c, 2816)
    for ci, (_, typ, _) in enumerate(QK_CHUNKS):
        add("qk", ci, 1024 if typ == "na" else 2048)
    for i in range(2):
        add("v", i, 3584)
    for i in range(4):
        add("wo", i, 2048)
    return tab, off


TILE_TAB, LAYER_W = layer_tile_table()


def lhsT_tile(w, c0, ncols=128):
    return w[:, c0:c0 + ncols].reshape(8, 128, ncols).transpose(1, 0, 2)


def prep_weights(inp):
    wall = np.empty((128, L * LAYER_W), np.float32)
    for l in range(L):
        base = l * LAYER_W
        for ff, (ng, nu, nd) in (("f1", ("ffn1_w_gate", "ffn1_w_up", "ffn1_w_down")),
                                 ("f2", ("ffn2_w_gate", "ffn2_w_up", "ffn2_w_down"))):
            wg, wu, wd = inp[ng][l], inp[nu][l], inp[nd][l]
            for fc in range(NFC):
                o, n = TILE_TAB[(ff + "gu", fc)]
                t = np.stack([lhsT_tile(wg, fc * 128), lhsT_tile(wu, fc * 128)], axis=1)
                wall[:, base + o:base + o + n] = t.reshape(128, n)
            wd3 = wd.reshape(NFC, 128, D)
            for c in range(8):
                o, n = TILE_TAB[(ff + "d", c)]
                t = wd3[:, :, c * 128:(c + 1) * 128].transpose(1, 0, 2)
                wall[:, base + o:base + o + n] = t.reshape(128, n)
        w_in = inp["w_in"][l]
        for ci, (c0, typ, _) in enumerate(QK_CHUNKS):
            o, n = TILE_TAB[("qk", ci)]
            main = lhsT_tile(w_in, c0)
            if typ == "na":
                wall[:, base + o:base + o + n] = main.reshape(128, n)
            else:
                d = HD[typ]
                sw = swap_index(typ)
                cols = c0 + (np.arange(128) // d) * d + sw[np.arange(128) % d]
                swp = w_in[:, cols].reshape(8, 128, 128).transpose(1, 0, 2)
                wall[:, base + o:base + o + n] = np.stack([main, swp], axis=1).reshape(128, n)
        wv = np.concatenate([w_in[:, c0:c0 + n_] for c0, n_ in V_COLS], axis=1)
        wv3 = wv.reshape(8, 128, 896).transpose(1, 0, 2)
        for i in range(2):
            o, n = TILE_TAB[("v", i)]
            wall[:, base + o:base + o + n] = wv3[:, :, i * 448:(i + 1) * 448].reshape(128, n)
        wo = inp["w_out"][l]
        for i in range(4):
            o, n = TILE_TAB[("wo", i)]
            t = np.stack([lhsT_tile(wo, (2 * i) * 128), lhsT_tile(wo, (2 * i + 1) * 128)], axis=1)
            wall[:, base + o:base + o + n] = t.reshape(128, n)
    return wall


def prep_vecs(inp):
    vecs = np.zeros((128, L * NV), np.float32)
    p = np.arange(128)
    for l in range(L):
        b = l * NV
        vecs[:, b + 0:b + 8] = inp["ffn1_norm"][l].reshape(8, 128).T
        vecs[:, b + 8:b + 16] = inp["mix_norm"][l].reshape(8, 128).T
        vecs[:, b + 16:b + 24] = inp["ffn2_norm"][l].reshape(8, 128).T
        names = {("na", 0): "na_q_norm", ("na", 1): "na_k_norm", ("diff", 0): "diff_q_norm",
                 ("diff", 1): "diff_k_norm", ("gqa", 0): "gqa_q_norm", ("gqa", 1): "gqa_k_norm",
                 ("dil", 0): "dil_q_norm", ("dil", 1): "dil_k_norm"}
        for key, nm in names.items():
            g = inp[nm][l]
            d = HD[key[0]]
            c_main, c_sw = GCOL[key]
            vecs[:, b + c_main] = g[p % d]
            if c_sw is not None:
                vecs[:, b + c_sw] = g[swap_index(key[0])[p % d]]
        vecs[:, b + 38] = inp["diff_out_norm"][l][p % 64]
    lam = np.zeros((64, L * 128), np.float32)
    for l in range(L):
        for i, nm in enumerate(("diff_lambda_q1", "diff_lambda_k1", "diff_lambda_q2", "diff_lambda_k2")):
            lam[:, l * 128 + i * 32:l * 128 + (i + 1) * 32] = np.broadcast_to(inp[nm][l][None, :], (64, 32))
    rb = inp["na_rel_bias"]
    kc = np.arange(64)[:, None]
    qc = np.arange(64)[None, :]
    co = np.clip(kc - qc + 15, 0, 30)
    natab = np.zeros((L, 4, 23, 64, 64), np.float32)
    for a in range(23):
        dr = 18 - a
        if 0 <= dr <= 14:
            natab[:, :, a] = rb[:, :, dr][:, :, co]
    return vecs, lam, natab


def rope_tables_core(half):
    pos = (half * TOK + np.arange(TOK)).astype(np.int32)
    out = np.zeros((128, 6, TOK), np.float32)
    p = np.arange(128)

    def tables(posv, dim, theta):
        inv = (1.0 / (np.float32(theta) ** (np.arange(0, dim, 2, dtype=np.float32) / np.float32(dim)))).astype(np.float32)
        ang = posv.astype(np.float32)[:, None] * inv[None, :]
        return np.cos(ang).astype(np.float32), np.sin(ang).astype(np.float32)
    c, s = tables(pos, 8, 500000.0)
    j = p % 32
    for pp in range(128):
        jj = j[pp]
        if jj < 8:
            out[pp, 0] = c[:, jj % 4]
            out[pp, 1] = (-s[:, jj % 4]) if jj < 4 else s[:, jj % 4]
        else:
            out[pp, 0] = 1.0
    cr, sr = tables(pos // 64, 32, 10000.0)
    cc, sc = tables(pos % 64, 32, 10000.0)
    j = p % 64
    for pp in range(128):
        jj = j[pp] % 32
        cT, sT = (cr, sr) if j[pp] < 32 else (cc, sc)
        out[pp, 2] = cT[:, jj % 16]
        out[pp, 3] = (-sT[:, jj % 16]) if jj < 16 else sT[:, jj % 16]
    c, s = tables(pos, 16, 500000.0)
    for pp in range(128):
        jj = j[pp]
        if jj < 16:
            out[pp, 4] = c[:, jj % 8]
            out[pp, 5] = (-s[:, jj % 8]) if jj < 8 else s[:, jj % 8]
        else:
            out[pp, 4] = 1.0
    return np.ascontiguousarray(out.reshape(128, 6, NG, G).transpose(2, 0, 1, 3))


def na_rowmask(r0):
    out = np.full((8, 2, 64, 8, 64), NEG, np.float32)
    kc = np.arange(64)[:, None]
    qc = np.arange(64)[None, :]
    cs = np.clip(qc - 8, 0, 48)
    colok = (kc >= cs) & (kc < cs + 16)
    for j in range(8):
        for krl in range(2):
            kr = r0 - 4 + 2 * j + krl
            if kr < 0 or kr > 127:
                continue
            for qrl in range(8):
                qr = r0 + qrl
                rs = min(max(qr - 4, 0), 120)
                if rs <= kr < rs + 8:
                    out[j, krl, :, qrl, :] = np.where(colok, 0.0, NEG)
    return out.reshape(8, 128, 512)


def dil_bias():
    p = np.arange(128)[:, None]
    q = np.arange(512)[None, :]
    out = np.zeros((20, 128, 512), np.float32)
    for i in range(20):
        dl = 128 * i - 1024 + p - q
        a = np.abs(dl)
        c = (a <= 64).astype(np.int32) + ((dl % 4 == 0) & (a <= 256)) + ((dl % 16 == 0) & (a <= 1024))
        out[i] = np.where(c > 0, np.log(np.maximum(c, 1).astype(np.float32)), NEG)
    return out.astype(np.float32)


def block_consts():
    bd = np.zeros((128, 3, 128), np.float32)
    bd[:, 0, :] = 1.0
    for b in range(2):
        bd[b * 64:(b + 1) * 64, 1, b * 64:(b + 1) * 64] = 1.0
    for b in range(4):
        bd[b * 32:(b + 1) * 32, 2, b * 32:(b + 1) * 32] = 1.0
    return bd.reshape(128, 384)


class Buf:
    __slots__ = ("name", "t", "w", "r")

    def __init__(self, name, t=None):
        self.name = name
        self.t = t
        self.w = []
        self.r = []

    def __getitem__(self, idx):
        return self.t[idx]


class Trk:
    def __init__(self, nc, n_dma_sems=40):
        self.nc = nc
        self.eng = {"pe": nc.tensor, "act": nc.scalar, "dve": nc.vector, "pool": nc.gpsimd, "sp": nc.sync}
        self.sem, self.cnt = {}, {}
        for k in ("pe", "act", "dve", "pool"):
            self.sem[k] = nc.alloc_semaphore("s_" + k)
            self.cnt[k] = 0
        self.nd = n_dma_sems
        for i in range(n_dma_sems):
            self.sem["d%d" % i] = nc.alloc_semaphore("d%d" % i)
            self.cnt["d%d" % i] = 0
        self.dnext = 0
        self.known = {e: {} for e in self.eng}

    def _need(self, e, key, count):
        if key == e and e == "pe":
            return
        kn = self.known[e]
        if kn.get(key, 0) >= count:
            return
        self.eng[e].wait_ge(self.sem[key], count)
        kn[key] = count

    @staticmethod
    def _compress(lst):
        mx = {}
        for k_, c_ in lst:
            if mx.get(k_, 0) < c_:
                mx[k_] = c_
        return list(mx.items())

    def _deps(self, e, reads, writes, acc=False):
        for b in reads:
            for ww in b.w:
                self._need(e, *ww)
        if acc:
            return
        for b in writes:
            for ww in b.w:
                self._need(e, *ww)
            for rr in b.r:
                self._need(e, *rr)

    def _mark(self, ev, reads, writes, acc=False):
        for b in reads:
            b.r.append(ev)
            if len(b.r) > 48:
                b.r = self._compress(b.r)
        for b in writes:
            if acc:
                b.w.append(ev)
                if len(b.w) > 48:
                    b.w = self._compress(b.w)
            else:
                b.w = [ev]
                b.r = []

    def op(self, e, fn, reads=(), writes=()):
        self._deps(e, reads, writes)
        ins = fn(self.eng[e])
        self.cnt[e] += 1
        ins.then_inc(self.sem[e], 1)
        ev = (e, self.cnt[e])
        self._mark(ev, reads, writes)
        return ev

    def dma(self, e, out, in_, reads=(), writes=(), acc=False, **kw):
        self._deps(e, reads, writes, acc)
        key = "d%d" % self.dnext
        self.dnext = (self.dnext + 1) % self.nd
        if self.cnt[key] > 0:
            self._need(e, key, self.cnt[key])
        ins = self.eng[e].dma_start(out=out, in_=in_, **kw)
        self.cnt[key] += 16
        ins.then_inc(self.sem[key], 16)
        ev = (key, self.cnt[key])
        self._mark(ev, reads, writes, acc)
        return ev

    def wait_all(self, e, bufs):
        for b in bufs:
            for ww in b.w:
                self._need(e, *ww)
            for rr in b.r:
                self._need(e, *rr)


class Ring:
    def __init__(self, bufs):
        self.bufs = bufs
        self.i = 0

    def next(self):
        b = self.bufs[self.i]
        self.i = (self.i + 1) % len(self.bufs)
        return b


class Prog:
    def __init__(self, stages, exchange="host"):
        self.stages = stages
        self.exchange = exchange
        nc = self.nc = bass.Bass("TRN2", target_bir_lowering=False)
        self.k = Trk(nc)
        self.dram = {}
        self.ext_in, self.ext_out = [], []
        sset = set(stages)
        self.layers_w = sorted({l for (_, l) in stages if _ in ("TA", "TB")})

        def dt(name, shape, dtype, kind):
            t = nc.dram_tensor(name, shape, dtype, kind=kind)
            self.dram[name] = Buf(name, t.ap())
            if kind == "ExternalInput":
                self.ext_in.append(name)
            if kind == "ExternalOutput":
                self.ext_out.append(name)
            return self.dram[name]
        first = stages[0]
        last = stages[-1]
        if self.layers_w:
            self.x_in = dt("x_in", [8, 128, TOK], F32, "ExternalInput")
            self.x_out = dt("x_out", [8, 128, TOK], F32, "ExternalOutput")
        if self.layers_w:
            self.wall = dt("wall", [128, len(self.layers_w) * LAYER_W], F32, "ExternalInput")
            self.ws = dt("ws", [128, len(self.layers_w) * LAYER_W], BF16, "Internal")
        self.nsl = max(l_ for (_, l_) in stages) + 1
        self.vecs = dt("vecs", [128, self.nsl * NV], F32, "ExternalInput")
        self.bdc = dt("bdc", [128, 384], F32, "ExternalInput")
        if any(s_[0] == "TA" for s_ in stages):
            self.tabs = dt("tabs", [NG, 128, 6 * G], F32, "ExternalInput")
        if any(s_[0] == "ATT" for s_ in stages):
            self.lam = dt("lam", [64, self.nsl * 130], F32, "ExternalInput")
            self.natab = dt("natab", [self.nsl * 4 * 23 * 64, 64], F32, "ExternalInput")
            self.rmc = dt("rmc", [24, 128, 512], F32, "ExternalInput")
            self.dilb = dt("dilb", [20, 128, 512], F32, "ExternalInput")
            self.vpn = dt("vpn", [128, 2], F32, "ExternalInput")
        self.qt, self.kto, self.vo, self.ktg, self.vg, self.mt = {}, {}, {}, {}, {}, {}
        for l in range(self.nsl):
            ta, att, tb = ("TA", l) in sset, ("ATT", l) in sset, ("TB", l) in sset
            if ta or att:
                kind_o = "Internal" if (ta and att) else ("ExternalOutput" if ta else "ExternalInput")
                self.qt[l] = dt("qt%d" % l, [8, 128, TOK], BF16, kind_o)
                if ta:
                    k2 = "Internal" if (att and exchange == "cc") else "ExternalOutput"
                    self.kto[l] = dt("kto%d" % l, [7 * 128, TOK], BF16, k2)
                    self.vo[l] = dt("vo%d" % l, [TOK, 896], BF16, k2)
                else:
                    self.kto[l] = dt("kto%d" % l, [7 * 128, TOK], BF16, "ExternalInput")
                    self.vo[l] = dt("vo%d" % l, [TOK, 896], BF16, "ExternalInput")
                if att:
                    k3 = "Internal" if (ta and exchange == "cc") else "ExternalInput"
                    self.ktg[l] = dt("ktg%d" % l, [2 * 7 * 128, TOK], BF16, k3)
                    self.vg[l] = dt("vg%d" % l, [2 * TOK, 896], BF16, k3)
            if att or tb:
                kind_m = "Internal" if (att and tb) else ("ExternalOutput" if att else "ExternalInput")
                self.mt[l] = dt("mt%d" % l, [8, 128, TOK], BF16, kind_m)
        if exchange == "cc":
            self.g8k = dt("g8k", [8 * 896, TOK], BF16, "Internal")
            self.g8v = dt("g8v", [8 * TOK, 896], BF16, "Internal")
        self.build()

    def exchange_kv(self, l):
        nc, k = self.nc, self.k
        g = nc.gpsimd
        if not hasattr(self, "cc_sem"):
            self.cc_sem = nc.alloc_semaphore("cc")
            self.cc_cnt = 0
            self.pid = g.partition_id()
        for src, g8, dst, rows in ((self.kto[l], self.g8k, self.ktg[l], 896), (self.vo[l], self.g8v, self.vg[l], TOK)):
            k._deps("pool", [src], [g8])
            g.collective_compute("AllGather", ALU.bypass, replica_groups=[list(range(8))],
                                 ins=[src[:, :]], outs=[g8[:, :]]).then_inc(self.cc_sem, 1)
            self.cc_cnt += 1
            g.wait_ge(self.cc_sem, self.cc_cnt)
            src.r.append(("pool", k.cnt["pool"]))
            g8.w = []
            g8.r = []
            base = (self.pid // 2) * (2 * rows)
            k.dma("pool", dst[:, :], g8[bass.ds(base, 2 * rows), :], reads=[g8], writes=[dst])

    def _nm(self, base):
        self._uid = getattr(self, "_uid", 0) + 1
        return "%s_%d" % (base, self._uid)

    def sb(self, name, shape, dtype):
        return Buf(name, self.nc.alloc_sbuf_tensor(name, shape, dtype))

    def build(self):
        nc, k = self.nc, self.k
        self.vec_sb = self.sb("vec_sb", [128, self.nsl * NV], F32)
        k.dma("sp", self.vec_sb[:], self.vecs[:, :], writes=[self.vec_sb])
        bdf = self.sb("bdf", [128, 384], F32)
        self.bd = self.sb("bd", [128, 384], BF16)
        k.dma("sp", bdf[:], self.bdc[:, :], writes=[bdf])
        k.op("dve", lambda e: e.tensor_copy(out=self.bd[:], in_=bdf[:]), reads=[bdf], writes=[self.bd])
        self.psum = Ring([Buf("ps%d" % i, nc.alloc_psum_tensor("ps%d" % i, [128, 512], F32)) for i in range(8)])
        self.cast_weights()
        i = 0
        st = self.stages
        while i < len(st):
            if st[i][0] in ("TA", "TB"):
                j = i
                while j < len(st) and st[j][0] in ("TA", "TB"):
                    j += 1
                self.t_phase(st[i:j])
                i = j
            else:
                if self.exchange == "cc":
                    self.exchange_kv(st[i][1])
                self.att_phase(st[i][1])
                i += 1
        k.wait_all("sp", [self.dram[n] for n in self.ext_out])

    def cast_weights(self):
        nc, k = self.nc, self.k
        tot = len(self.layers_w) * LAYER_W
        if tot == 0:
            return
        CH = 4096
        with nc.sbuf_tensor(self._nm("cst_f"), [128, 3, CH], F32) as cf_t, nc.sbuf_tensor(self._nm("cst_b"), [128, 3, CH], BF16) as cb_t:
            cf = [Buf("cf%d" % i, cf_t[:, i, :]) for i in range(3)]
            cb = [Buf("cb%d" % i, cb_t[:, i, :]) for i in range(3)]
            engs = ["dve", "pool", "act"]
            i = 0
            for c0 in range(0, tot, CH):
                n = min(CH, tot - c0)
                f, b = cf[i % 3], cb[i % 3]
                k.dma("sp", f.t[:, 0:n], self.wall[:, c0:c0 + n], writes=[f])
                e = engs[i % 3]
                if e == "act":
                    k.op(e, lambda en, f=f, b=b, n=n: en.copy(out=b.t[:, 0:n], in_=f.t[:, 0:n]), reads=[f], writes=[b])
                else:
                    k.op(e, lambda en, f=f, b=b, n=n: en.tensor_copy(out=b.t[:, 0:n], in_=f.t[:, 0:n]), reads=[f], writes=[b])
                k.dma("pool", self.ws[:, c0:c0 + n], b.t[:, 0:n], reads=[b], writes=[self.ws], acc=True)
                i += 1
            k.wait_all("sp", [self.ws])
            k.wait_all("pool", cf + cb)
            k.wait_all("dve", cf + cb)
            k.wait_all("act", cf + cb)
            k.wait_all("sp", cf + cb)

    def t_phase(self, tst):
        nc, k = self.nc, self.k
        vec = self.vec_sb
        with (nc.sbuf_tensor(self._nm("t_x"), [128, 2, 8, G], F32) as x_t,
              nc.sbuf_tensor(self._nm("t_m"), [128, 2, 8, G], BF16) as m_t,
              nc.sbuf_tensor(self._nm("t_h"), [128, 8, G], BF16) as h_t,
              nc.sbuf_tensor(self._nm("t_act"), [128, NFC, G], BF16) as act_t,
              nc.sbuf_tensor(self._nm("t_sq"), [128, 3, G], BF16) as sq_t,
              nc.sbuf_tensor(self._nm("t_f32"), [128, 8, G], F32) as f32_t,
              nc.sbuf_tensor(self._nm("t_qo"), [128, 3, G], BF16) as qo_t,
              nc.sbuf_tensor(self._nm("t_vo"), [128, 2, 896], BF16) as vo_t,
              nc.sbuf_tensor(self._nm("t_tab"), [128, 6, G], F32) as tab_t,
              nc.sbuf_tensor(self._nm("t_w"), [128, NSLOT, SLOT], BF16) as w_t):
            xb = [[Buf("x%d_%d" % (i, c), x_t[:, i, c, :]) for c in range(8)] for i in range(2)]
            mbb = [[Buf("m%d_%d" % (i, c), m_t[:, i, c, :]) for c in range(8)] for i in range(2)]
            hb = [Buf("h%d" % c, h_t[:, c, :]) for c in range(8)]
            actb = [Buf("act%d" % i, act_t[:, i, :]) for i in range(NFC)]
            sqr = Ring([Buf("sq%d" % i, sq_t[:, i, :]) for i in range(3)])
            f32r = Ring([Buf("f32_%d" % i, f32_t[:, i, :]) for i in range(8)])
            qor = Ring([Buf("qo%d" % i, qo_t[:, i, :]) for i in range(3)])
            vor = Ring([Buf("vo%d" % i, vo_t[:, i, :]) for i in range(2)])
            tabb = Buf("tab", tab_t[:])
            wslots = Ring([Buf("w%d" % i, w_t[:, i, :]) for i in range(NSLOT)])

            def stage_tiles(kind, l):
                lp = self.layers_w.index(l)
                seq = []
                if kind == "TB":
                    seq += [(lp, "wo", i) for i in range(4)]
                    seq += [(lp, "f2gu", i) for i in range(NFC)] + [(lp, "f2d", i) for i in range(8)]
                else:
                    seq += [(lp, "f1gu", i) for i in range(NFC)] + [(lp, "f1d", i) for i in range(8)]
                    seq += [(lp, "qk", i) for i in range(len(QK_CHUNKS))] + [(lp, "v", i) for i in range(2)]
                return seq
            wseq = []
            for g in range(NG):
                for (kind, l) in tst:
                    wseq += stage_tiles(kind, l)
            wstate = {"issued": 0, "taken": 0, "q": []}

            def w_issue():
                lp, name, idx = wseq[wstate["issued"]]
                o, n = TILE_TAB[(name, idx)]
                slot = wslots.next()
                k.dma("sp", slot.t[:, 0:n], self.ws[:, lp * LAYER_W + o:lp * LAYER_W + o + n],
                      reads=[self.ws], writes=[slot])
                wstate["q"].append((slot, name, idx))
                wstate["issued"] += 1

            def w_get(name, idx):
                while wstate["issued"] < len(wseq) and wstate["issued"] - wstate["taken"] < NSLOT - 1:
                    w_issue()
                slot, nm, ix = wstate["q"].pop(0)
                assert (nm, ix) == (name, idx), (nm, ix, name, idx)
                wstate["taken"] += 1
                return slot

            has_tb = tst[0][0] == "TB"
            l_tb = tst[0][1] if has_tb else None
            ta_list = [s for s in tst if s[0] == "TA"]
            l_ta = ta_list[0][1] if ta_list else None
            if has_tb:
                x_src = self.dram["x_in"] if ("TA", l_tb) not in set(self.stages) else self.x1
            else:
                x_src = self.dram["x_in"]
            if l_ta is not None:
                x_dst = self.dram["x_out"] if ("TB", l_ta) not in set(self.stages) else self._x1_tensor()
            else:
                x_dst = self.dram["x_out"]

            def load_group(g):
                xg = xb[g % 2]
                k.dma("pool", x_t[:, g % 2], x_src[:, :, g * G:(g + 1) * G].rearrange("c p t -> p c t"),
                      reads=[x_src], writes=xg)
                if has_tb:
                    k.dma("pool", m_t[:, g % 2], self.mt[l_tb][:, :, g * G:(g + 1) * G].rearrange("c p t -> p c t"),
                          reads=[self.mt[l_tb]], writes=mbb[g % 2])

            def norm(xg, gcol0):
                ss = self.psum.next()
                for c in range(8):
                    sq = sqr.next()
                    k.op("pool", lambda e, sq=sq, c=c: e.tensor_tensor(out=sq.t, in0=xg[c].t, in1=xg[c].t, op=ALU.mult),
                         reads=[xg[c]], writes=[sq])
                    k.op("pe", lambda e, sq=sq, c=c: e.matmul(ss.t[:, :], lhsT=self.bd[:, 0:128], rhs=sq.t, start=(c == 0), stop=(c == 7)),
                         reads=[sq, self.bd], writes=[ss])
                rs = f32r.next()
                k.op("act", lambda e: e.activation(out=rs.t, in_=ss.t[:, :], func=AF.Sqrt, scale=1.0 / D, bias=EPS),
                     reads=[ss], writes=[rs])
                k.op("dve", lambda e: e.reciprocal(out=rs.t, in_=rs.t), reads=[rs], writes=[rs])
                for c in range(8):
                    k.op("dve", lambda e, c=c: e.scalar_tensor_tensor(out=hb[c].t, in0=xg[c].t,
                                                                      scalar=vec[:, gcol0 + c:gcol0 + c + 1], in1=rs.t,
                                                                      op0=ALU.mult, op1=ALU.mult),
                         reads=[xg[c], rs, vec], writes=[hb[c]])

            def ffn(xg, ff):
                for fc in range(NFC):
                    w = w_get(ff + "gu", fc)
                    wv_ = w.t[:, 0:2048].rearrange("p (a k f) -> p a k f", a=2, k=8)
                    gp, up = self.psum.next(), self.psum.next()
                    for a, pt in ((0, gp), (1, up)):
                        for kk in range(8):
                            k.op("pe", lambda e, a=a, pt=pt, kk=kk: e.matmul(pt.t[:, :], lhsT=wv_[:, a, kk, :], rhs=hb[kk].t,
                                                                            start=(kk == 0), stop=(kk == 7)),
                                 reads=[w, hb[kk]], writes=[pt])
                    sg = f32r.next()
                    k.op("act", lambda e: e.activation(out=sg.t, in_=gp.t[:, :], func=AF.Silu), reads=[gp], writes=[sg])
                    k.op("dve", lambda e, fc=fc: e.tensor_tensor(out=actb[fc].t, in0=sg.t, in1=up.t[:, :], op=ALU.mult),
                         reads=[sg, up], writes=[actb[fc]])
                for c in range(8):
                    w = w_get(ff + "d", c)
                    wv_ = w.t[:, 0:2816].rearrange("p (k f) -> p k f", k=NFC)
                    yp = self.psum.next()
                    for fk in range(NFC):
                        k.op("pe", lambda e, fk=fk: e.matmul(yp.t[:, :], lhsT=wv_[:, fk, :], rhs=actb[fk].t,
                                                             start=(fk == 0), stop=(fk == NFC - 1)),
                             reads=[w, actb[fk]], writes=[yp])
                    k.op("dve", lambda e, c=c: e.scalar_tensor_tensor(out=xg[c].t, in0=yp.t[:, :], scalar=0.5,
                                                                      in1=xg[c].t, op0=ALU.mult, op1=ALU.add),
                         reads=[yp, xg[c]], writes=[xg[c]])

            def out_proj(xg, mb):
                for i in range(4):
                    w = w_get("wo", i)
                    wv_ = w.t[:, 0:2048].rearrange("p (a k f) -> p a k f", a=2, k=8)
                    for a in range(2):
                        c = 2 * i + a
                        yp = self.psum.next()
                        for kk in range(8):
                            k.op("pe", lambda e, a=a, kk=kk: e.matmul(yp.t[:, :], lhsT=wv_[:, a, kk, :], rhs=mb[kk].t,
                                                                      start=(kk == 0), stop=(kk == 7)),
                                 reads=[w, mb[kk]], writes=[yp])
                        k.op("dve", lambda e, c=c: e.tensor_tensor(out=xg[c].t, in0=yp.t[:, :], in1=xg[c].t, op=ALU.add),
                             reads=[yp, xg[c]], writes=[xg[c]])

            def in_proj(g, l):
                vb = l * NV
                k.dma("pool", tabb.t, self.tabs[g].rearrange("p (a t) -> p a t", a=6), reads=[self.tabs], writes=[tabb])
                qt_d, kt_d, v_d = self.qt[l], self.kto[l], self.vo[l]
                for ci, (_, typ, is_k) in enumerate(QK_CHUNKS):
                    w = w_get("qk", ci)
                    roped = typ != "na"
                    d = HD[typ]
                    if roped:
                        wv_ = w.t[:, 0:2048].rearrange("p (a k f) -> p a k f", a=2, k=8)
                    else:
                        wv_ = w.t[:, 0:1024].rearrange("p (a k f) -> p a k f", a=1, k=8)
                    pm = self.psum.next()
                    psw = self.psum.next() if roped else None
                    for a, pt in ((0, pm), (1, psw)):
                        if pt is None:
                            continue
                        for kk in range(8):
                            k.op("pe", lambda e, a=a, pt=pt, kk=kk: e.matmul(pt.t[:, :], lhsT=wv_[:, a, kk, :], rhs=hb[kk].t,
                                                                            start=(kk == 0), stop=(kk == 7)),
                                 reads=[w, hb[kk]], writes=[pt])
                    sq = sqr.next()
                    k.op("act", lambda e: e.activation(out=sq.t, in_=pm.t[:, :], func=AF.Square), reads=[pm], writes=[sq])
                    ss = self.psum.next()
                    bsel = 1 if d == 64 else 2
                    k.op("pe", lambda e: e.matmul(ss.t[:, :], lhsT=self.bd[:, bsel * 128:(bsel + 1) * 128], rhs=sq.t, start=True, stop=True),
                         reads=[sq, self.bd], writes=[ss])
                    rs = f32r.next()
                    k.op("act", lambda e: e.activation(out=rs.t, in_=ss.t[:, :], func=AF.Sqrt, scale=1.0 / d, bias=EPS),
                         reads=[ss], writes=[rs])
                    k.op("dve", lambda e: e.reciprocal(out=rs.t, in_=rs.t), reads=[rs], writes=[rs])
                    gm, gs = GCOL[(typ, is_k)]
                    qo = qor.next()
                    if not roped:
                        k.op("dve", lambda e: e.scalar_tensor_tensor(out=qo.t, in0=pm.t[:, :], scalar=vec[:, vb + gm:vb + gm + 1],
                                                                     in1=rs.t, op0=ALU.mult, op1=ALU.mult),
                             reads=[pm, rs, vec], writes=[qo])
                    else:
                        ti = TABI[typ]
                        t1, t2 = f32r.next(), f32r.next()
                        k.op("dve", lambda e: e.scalar_tensor_tensor(out=t1.t, in0=pm.t[:, :], scalar=vec[:, vb + gm:vb + gm + 1],
                                                                     in1=tabb.t[:, ti, :], op0=ALU.mult, op1=ALU.mult),
                             reads=[pm, tabb, vec], writes=[t1])
                        k.op("dve", lambda e: e.scalar_tensor_tensor(out=t2.t, in0=psw.t[:, :], scalar=vec[:, vb + gs:vb + gs + 1],
                                                                     in1=tabb.t[:, ti + 1, :], op0=ALU.mult, op1=ALU.mult),
                             reads=[psw, tabb, vec], writes=[t2])
                        k.op("pool", lambda e: e.tensor_tensor(out=t1.t, in0=t1.t, in1=t2.t, op=ALU.add), reads=[t1, t2], writes=[t1])
                        k.op("pool", lambda e: e.tensor_tensor(out=qo.t, in0=t1.t, in1=rs.t, op=ALU.mult), reads=[t1, rs], writes=[qo])
                    if not is_k:
                        k.dma("pool", qt_d[ci, :, g * G:(g + 1) * G], qo.t, reads=[qo], writes=[qt_d], acc=True)
                    else:
                        r0 = (ci - 8) * 128
                        k.dma("pool", kt_d[r0:r0 + 128, g * G:(g + 1) * G], qo.t, reads=[qo], writes=[kt_d], acc=True)
                wv0, wv1 = w_get("v", 0), w_get("v", 1)
                wvs = [wv0.t[:, 0:3584].rearrange("p (k f) -> p k f", k=8), wv1.t[:, 0:3584].rearrange("p (k f) -> p k f", k=8)]
                for tt in range(4):
                    vo_b = vor.next()
                    for hf in range(2):
                        vp_ = self.psum.next()
                        for kk in range(8):
                            k.op("pe", lambda e, hf=hf, kk=kk, tt=tt: e.matmul(vp_.t[:, 0:448], lhsT=hb[kk].t[:, tt * 128:(tt + 1) * 128],
                                                                               rhs=wvs[hf][:, kk, :], start=(kk == 0), stop=(kk == 7)),
                                 reads=[(wv0, wv1)[hf], hb[kk]], writes=[vp_])
                        k.op("act", lambda e, hf=hf: e.copy(out=vo_b.t[:, hf * 448:(hf + 1) * 448], in_=vp_.t[:, 0:448]),
                             reads=[vp_], writes=[vo_b])
                    t0 = g * G + tt * 128
                    k.dma("pool", v_d[t0:t0 + 128, :], vo_b.t, reads=[vo_b], writes=[v_d], acc=True)

            load_group(0)
            for g in range(NG):
                xg = xb[g % 2]
                if g + 1 < NG:
                    load_group(g + 1)
                for (kind, l) in tst:
                    vb = l * NV
                    if kind == "TB":
                        out_proj(xg, mbb[g % 2])
                        norm(xg, vb + 16)
                        ffn(xg, "f2")
                    else:
                        norm(xg, vb + 0)
                        ffn(xg, "f1")
                        norm(xg, vb + 8)
                        in_proj(g, l)
                k.dma("pool", x_dst[:, :, g * G:(g + 1) * G].rearrange("c p t -> p c t"), x_t[:, g % 2], reads=xg, writes=[x_dst])
            allb = xb[0] + xb[1] + mbb[0] + mbb[1] + hb + actb + sqr.bufs + f32r.bufs + qor.bufs + vor.bufs + [tabb] + wslots.bufs
            for e in ("sp", "pool", "act", "dve", "pe"):
                k.wait_all(e, allb)

    def _x1_tensor(self):
        if not hasattr(self, "x1"):
            t = self.nc.dram_tensor("x1s", [8, 128, TOK], F32, kind="Internal")
            self.x1 = Buf("x1s", t.ap())
        return self.x1

    def att_phase(self, l):
        nc, k = self.nc, self.k
        vec = self.vec_sb
        vb = l * NV
        ktg, vg, kto, vo, qt, mt = self.ktg[l], self.vg[l], self.kto[l], self.vo[l], self.qt[l], self.mt[l]
        ps = self.psum.bufs
        s_ring, o_ring = Ring(ps[0:4]), Ring(ps[4:8])
        m_ring = s_ring
        with (nc.sbuf_tensor(self._nm("a_kt"), [128, 8192], BF16) as kt_t,
              nc.sbuf_tensor(self._nm("a_va"), [128, 64, 128], BF16) as va_t,
              nc.sbuf_tensor(self._nm("a_q"), [128, TOK], BF16) as q_t,
              nc.sbuf_tensor(self._nm("a_p"), [128, 8, 512], BF16) as p_t,
              nc.sbuf_tensor(self._nm("a_s"), [128, 6, 512], F32) as s_t,
              nc.sbuf_tensor(self._nm("a_bias"), [128, 32, 512], F32) as b_t,
              nc.sbuf_tensor(self._nm("a_post"), [128, 8, 512], F32) as post_t,
              nc.sbuf_tensor(self._nm("a_ob"), [64, 3, 512], BF16) as ob_t,
              nc.sbuf_tensor(self._nm("a_small"), [64, 160], F32) as sm_t,
              nc.sbuf_tensor(self._nm("a_vpn"), [128, 2], F32) as vpn_t):
            ktb = Buf("kt", kt_t[:])
            vab = Buf("va", va_t[:])
            vones = Buf("vones", None)
            qb = Buf("q", q_t[:])
            pr = Ring([Buf("p%d" % i, p_t[:, i, :]) for i in range(8)])
            sr = Ring([Buf("s%d" % i, s_t[:, i, :]) for i in range(6)])
            bias = [Buf("b%d" % i, b_t[:, i, :]) for i in range(32)]
            postr = Ring([Buf("po%d" % i, post_t[:, i, :]) for i in range(8)])
            obr = Ring([Buf("ob%d" % i, ob_t[:, i, :]) for i in range(3)])
            smb = Buf("sm", sm_t[:])
            vpnb = Buf("vpn", vpn_t[:])
            k.dma("sp", vpn_t[:], self.vpn[:, :], writes=[vpnb])
            k.dma("sp", sm_t[:, 0:130], self.lam[:, l * 130:(l + 1) * 130], writes=[smb])
            k.op("dve", lambda e: e.tensor_tensor(out=sm_t[:, 0:32], in0=sm_t[:, 0:32], in1=sm_t[:, 32:64], op=ALU.mult), reads=[smb], writes=[smb])
            k.op("dve", lambda e: e.tensor_tensor(out=sm_t[:, 64:96], in0=sm_t[:, 64:96], in1=sm_t[:, 96:128], op=ALU.mult), reads=[smb], writes=[smb])
            k.op("dve", lambda e: e.reduce_sum(out=sm_t[:, 130:131], in_=sm_t[:, 0:32], axis=mybir.AxisListType.X), reads=[smb], writes=[smb])
            k.op("dve", lambda e: e.reduce_sum(out=sm_t[:, 131:132], in_=sm_t[:, 64:96], axis=mybir.AxisListType.X), reads=[smb], writes=[smb])
            k.op("act", lambda e: e.activation(out=sm_t[:, 132:134], in_=sm_t[:, 130:132], func=AF.Exp), reads=[smb], writes=[smb])
            k.op("dve", lambda e: e.tensor_tensor(out=sm_t[:, 134:135], in0=sm_t[:, 133:134], in1=sm_t[:, 132:133], op=ALU.subtract), reads=[smb], writes=[smb])
            k.op("dve", lambda e: e.tensor_tensor(out=sm_t[:, 134:135], in0=sm_t[:, 134:135], in1=sm_t[:, 128:129], op=ALU.add), reads=[smb], writes=[smb])
            k.op("dve", lambda e: e.tensor_tensor(out=sm_t[:, 135:136], in0=vec[0:64, vb + 38:vb + 39], in1=sm_t[:, 129:130], op=ALU.mult), reads=[smb, vec], writes=[smb])
            k.op("pool", lambda e: e.memset(va_t[:, :, 64:128], 1.0), writes=[vones])
            k.op("pool", lambda e: e.tensor_scalar(out=va_t[:, 0:8, 64:128], in0=va_t[:, 0:8, 64:128], scalar1=vpn_t[:, 0:1], scalar2=1.0, op0=ALU.mult, op1=ALU.mult),
                 reads=[vpnb, vones], writes=[vones])
            k.op("pool", lambda e: e.tensor_scalar(out=va_t[:, 40:48, 64:128], in0=va_t[:, 40:48, 64:128], scalar1=vpn_t[:, 1:2], scalar2=1.0, op0=ALU.mult, op1=ALU.mult),
                 reads=[vpnb, vones], writes=[vones])

            def load_head(krow, qrow, ext, pair=False):
                qc, qp = qrow // 128, qrow % 128
                if pair:
                    k.dma("sp", q_t[:, :], qt[qc, :, :], reads=[qt], writes=[qb])
                else:
                    k.dma("sp", q_t[0:64, :], qt[qc, qp:qp + 64, :], reads=[qt], writes=[qb])
                if ext:
                    k.dma("sp", kt_t[0:64, 0:1024], ktg[krow:krow + 64, 3072:4096], reads=[ktg], writes=[ktb])
                    k.dma("sp", kt_t[0:64, 1024:5120], kto[krow:krow + 64, :], reads=[kto], writes=[ktb], acc=True)
                    k.dma("sp", kt_t[0:64, 5120:6144], ktg[896 + krow:896 + krow + 64, 0:1024], reads=[ktg], writes=[ktb], acc=True)
                    k.dma("sp", va_t[:, 0:8, 0:64], vg[3072:4096, krow:krow + 64].rearrange("(c p) d -> p c d", p=128),
                          reads=[vg], writes=[vab])
                    k.dma("sp", va_t[:, 8:40, 0:64], vo[:, krow:krow + 64].rearrange("(c p) d -> p c d", p=128),
                          reads=[vo], writes=[vab], acc=True)
                    k.dma("sp", va_t[:, 40:48, 0:64], vg[TOK:TOK + 1024, krow:krow + 64].rearrange("(c p) d -> p c d", p=128),
                          reads=[vg], writes=[vab], acc=True)
                    k.op("pool", lambda e: e.tensor_scalar(out=va_t[:, 0:8, 0:64], in0=va_t[:, 0:8, 0:64], scalar1=vpn_t[:, 0:1], scalar2=1.0, op0=ALU.mult, op1=ALU.mult),
                         reads=[vab, vpnb], writes=[vab])
                    k.op("pool", lambda e: e.tensor_scalar(out=va_t[:, 40:48, 0:64], in0=va_t[:, 40:48, 0:64], scalar1=vpn_t[:, 1:2], scalar2=1.0, op0=ALU.mult, op1=ALU.mult),
                         reads=[vab, vpnb], writes=[vab])
                else:
                    for r in range(2):
                        k.dma("sp", kt_t[0:64, r * TOK:(r + 1) * TOK], ktg[r * 896 + krow:r * 896 + krow + 64, :], reads=[ktg], writes=[ktb], acc=(r == 1))
                        if pair:
                            k.dma("sp", kt_t[64:128, r * TOK:(r + 1) * TOK], ktg[r * 896 + krow:r * 896 + krow + 64, :], reads=[ktg], writes=[ktb], acc=True)
                        k.dma("sp", va_t[:, r * 32:(r + 1) * 32, 0:64], vg[r * TOK:(r + 1) * TOK, krow:krow + 64].rearrange("(c p) d -> p c d", p=128),
                              reads=[vg], writes=[vab], acc=(r == 1))

            def attend(qg, chunks, scale, maps, bias_fn=None, rm_fn=None):
                Os = [o_ring.next() for _ in maps]
                n = len(chunks)
                depth = 1 if len(maps) > 1 else 2
                pbufs = {}
                for i in range(n + depth):
                    if i < n:
                        kc = chunks[i]
                        for mi, (p0, kd) in enumerate(maps):
                            sp_ = s_ring.next()
                            k.op("pe", lambda e, kc=kc, sp_=sp_, p0=p0, kd=kd: e.matmul(
                                sp_.t[:, :], lhsT=kt_t[p0:p0 + kd, kc * 128:(kc + 1) * 128],
                                rhs=q_t[p0:p0 + kd, qg * G:(qg + 1) * G], start=True, stop=True),
                                 reads=[ktb, qb], writes=[sp_])
                            pb = pr.next()
                            if bias_fn is None:
                                k.op("act", lambda e, sp_=sp_, pb=pb: e.activation(out=pb.t, in_=sp_.t[:, :], func=AF.Exp, scale=scale),
                                     reads=[sp_], writes=[pb])
                            else:
                                bb = bias_fn(i)
                                s2 = sr.next()
                                k.op("dve", lambda e, sp_=sp_, s2=s2, bb=bb: e.scalar_tensor_tensor(out=s2.t, in0=sp_.t[:, :], scalar=scale, in1=bb.t,
                                                                                                   op0=ALU.mult, op1=ALU.add),
                                     reads=[sp_, bb], writes=[s2])
                                if rm_fn is not None:
                                    rb_ = rm_fn(i)
                                    k.op("pool", lambda e, s2=s2, rb_=rb_: e.tensor_tensor(out=s2.t, in0=s2.t, in1=rb_.t, op=ALU.add),
                                         reads=[s2, rb_], writes=[s2])
                                k.op("act", lambda e, s2=s2, pb=pb: e.activation(out=pb.t, in_=s2.t, func=AF.Exp), reads=[s2], writes=[pb])
                            pbufs[(mi, i)] = pb
                    j = i - depth
                    if j >= 0:
                        kc = chunks[j]
                        for mi in range(len(maps)):
                            k.op("pe", lambda e, kc=kc, j=j, mi=mi: e.matmul(Os[mi].t[:, :], lhsT=va_t[:, kc, :], rhs=pbufs[(mi, j)].t,
                                                                             start=(j == 0), stop=(j == n - 1)),
                                 reads=[vab, vones, pbufs[(mi, j)]], writes=[Os[mi]])
                return Os

            def normalize(O, dst_ap, dst_buf):
                rsb = postr.next()
                k.op("act", lambda e: e.copy(out=rsb.t[64:128, :], in_=O.t[64:128, :]), reads=[O], writes=[rsb])
                k.op("dve", lambda e: e.reciprocal(out=rsb.t[64:128, :], in_=rsb.t[64:128, :]), reads=[rsb], writes=[rsb])
                k.op("dve", lambda e: e.tensor_tensor(out=dst_ap, in0=O.t[0:64, :], in1=rsb.t[64:128, :], op=ALU.mult),
                     reads=[O, rsb], writes=[dst_buf])

            def store(ob, orow, qg):
                oc, op_ = orow // 128, orow % 128
                k.dma("pool", mt[oc, op_:op_ + 64, qg * G:(qg + 1) * G], ob.t, reads=[ob], writes=[mt], acc=True)

            def simple_head(orow, chunks_fn, scale, bias_fn=None, rm_fn=None):
                for qg in range(NG):
                    O = attend(qg, chunks_fn(qg), scale, [(0, 64)],
                               (lambda i, qg=qg: bias_fn(qg, i)) if bias_fn else None,
                               (lambda i, qg=qg: rm_fn(qg, i)) if rm_fn else None)[0]
                    ob = obr.next()
                    normalize(O, ob.t, ob)
                    store(ob, orow, qg)

            k.dma("sp", b_t[:, 8:32, :], self.rmc[:, :, :].rearrange("n p t -> p n t"), writes=bias[8:32])
            for h in range(4):
                load_head(64 * h, 64 * h, True)
                for j in range(8):
                    for krl in range(2):
                        a0 = 15 - 2 * j - krl
                        src = self.natab[((l * 4 + h) * 23 + a0) * 64:((l * 4 + h) * 23 + a0 + 8) * 64, :].rearrange("(a kc) qc -> kc a qc", kc=64)
                        k.dma("sp", b_t[krl * 64:(krl + 1) * 64, j, :].rearrange("p (a q) -> p a q", a=8), src,
                              reads=[self.natab], writes=[bias[j]], acc=(krl == 1))
                simple_head(64 * h, lambda qg: [6 + 4 * qg + j for j in range(8)], 0.125,
                            bias_fn=lambda qg, i: bias[i],
                            rm_fn=lambda qg, i: bias[8 + (0 if qg == 0 else (2 if qg == NG - 1 else 1)) * 8 + i])
            k.dma("sp", b_t[:, 0:20, :], self.dilb[:, :, :].rearrange("n p t -> p n t"), writes=bias[0:20])
            for h in range(4):
                load_head(640 + 64 * h, 768 + 64 * h, True)
                simple_head(768 + 64 * h, lambda qg: [4 * qg + i for i in range(20)], 0.125, bias_fn=lambda qg, i: bias[i])
            k.op("pool", lambda e: e.memset(va_t[:, :, 64:128], 1.0), reads=[vab], writes=[vones])
            allc = list(range(64))
            for n_ in range(2):
                load_head(512 + 64 * n_, 512 + 128 * n_, False, pair=True)
                for qg in range(NG):
                    Os = attend(qg, allc, 0.125, [(0, 64), (64, 64)])
                    for mi in range(2):
                        ob = obr.next()
                        normalize(Os[mi], ob.t, ob)
                        store(ob, 512 + 128 * n_ + 64 * mi, qg)
            dsc = 32 ** -0.5
            for h in range(4):
                load_head(256 + 64 * h, 256 + 64 * h, False)
                for qg in range(NG):
                    o12 = []
                    Os = attend(qg, allc, dsc, [(0, 32), (32, 32)])
                    for mp in range(2):
                        ob_ = postr.next()
                        normalize(Os[mp], ob_.t[0:64, :], ob_)
                        o12.append(ob_)
                    od = postr.next()
                    k.op("dve", lambda e: e.scalar_tensor_tensor(out=od.t[0:64, :], in0=o12[1].t[0:64, :], scalar=sm_t[:, 134:135], in1=o12[0].t[0:64, :],
                                                                 op0=ALU.mult, op1=ALU.add), reads=[o12[0], o12[1], smb], writes=[od])
                    sq = pr.next()
                    k.op("pool", lambda e: e.tensor_tensor(out=sq.t[0:64, :], in0=od.t[0:64, :], in1=od.t[0:64, :], op=ALU.mult), reads=[od], writes=[sq])
                    ss = m_ring.next()
                    k.op("pe", lambda e: e.matmul(ss.t[0:64, :], lhsT=self.bd[0:64, 128:192], rhs=sq.t[0:64, :], start=True, stop=True),
                         reads=[sq, self.bd], writes=[ss])
                    ln = postr.next()
                    k.op("act", lambda e: e.activation(out=ln.t[0:64, :], in_=ss.t[0:64, :], func=AF.Ln, scale=1.0 / 64, bias=EPS), reads=[ss], writes=[ln])
                    k.op("act", lambda e: e.activation(out=ln.t[0:64, :], in_=ln.t[0:64, :], func=AF.Exp, scale=-0.5), reads=[ln], writes=[ln])
                    ob = obr.next()
                    k.op("dve", lambda e: e.scalar_tensor_tensor(out=ob.t, in0=od.t[0:64, :], scalar=sm_t[:, 135:136], in1=ln.t[0:64, :],
                                                                 op0=ALU.mult, op1=ALU.mult), reads=[od, ln, smb], writes=[ob])
                    store(ob, 256 + 64 * h, qg)
            allb = [ktb, vab, vones, qb, smb, vpnb] + pr.bufs + sr.bufs + bias + postr.bufs + obr.bufs
            for e in ("sp", "pool", "act", "dve", "pe"):
                k.wait_all(e, allb)


def _core_inputs(inp):
    wall = prep_weights(inp)
    vecs, lam, natab = prep_vecs(inp)
    dilb = dil_bias()
    bdc = block_consts()
    rm_first, rm_int, rm_last = na_rowmask(0), na_rowmask(16), na_rowmask(120)
    per_core = []
    for c in range(8):
        b, half = c // 2, c % 2
        xT = np.ascontiguousarray(inp["x"][b, half * TOK:(half + 1) * TOK, :].T).reshape(8, 128, TOK)
        rmc = np.stack([rm_first if half == 0 else rm_int, rm_int, rm_last if half == 1 else rm_int]).reshape(24, 128, 512)
        vpn = np.zeros((128, 2), np.float32)
        vpn[:, 0] = 1.0 if half == 1 else 0.0
        vpn[:, 1] = 1.0 if half == 0 else 0.0
        per_core.append({"x_in": xT, "vecs": vecs, "lam": lam, "natab": natab.reshape(-1, 64),
                         "tabs": rope_tables_core(half).reshape(NG, 128, 6 * G), "rmc": rmc, "dilb": dilb,
                         "bdc": bdc, "vpn": vpn})
    return wall, per_core


def kernel(**inputs):
    inp = {k_: np.asarray(v) for k_, v in inputs.items()}
    wall, per_core = _core_inputs(inp)
    lam_all = per_core[0]["lam"]
    lam_pack = np.zeros((64, L * 130), np.float32)
    for l in range(L):
        lam_init = 0.8 - 0.6 * math.exp(-0.3 * l)
        lam_pack[:, l * 130:l * 130 + 128] = lam_all[:, l * 128:(l + 1) * 128]
        lam_pack[:, l * 130 + 128] = -lam_init
        lam_pack[:, l * 130 + 129] = 1.0 - lam_init
    stages = []
    for l in range(L):
        stages += [("TA", l), ("ATT", l), ("TB", l)]
    prog = Prog(stages, exchange="cc")
    maps = []
    for c in range(8):
        pc = per_core[c]
        maps.append({"x_in": pc["x_in"], "wall": wall, "vecs": pc["vecs"], "bdc": pc["bdc"], "tabs": pc["tabs"],
                     "lam": lam_pack, "natab": pc["natab"], "rmc": pc["rmc"], "dilb": pc["dilb"], "vpn": pc["vpn"]})
    res = run_bass_kernel_spmd(prog.nc, maps, core_ids=list(range(8)))
    out = np.empty((4, S, D), np.float32)
    for c in range(8):
        b, half = c // 2, c % 2
        out[b, half * TOK:(half + 1) * TOK, :] = np.asarray(res.results[c]["x_out"], np.float32).reshape(D, TOK).T
    return out
```

```python
import math
import numpy as np
import concourse.bass as bass
import concourse.mybir as mybir
from concourse.bass_utils import run_bass_kernel_spmd

F32 = mybir.dt.float32
BF16 = mybir.dt.bfloat16
AF = mybir.ActivationFunctionType
ALU = mybir.AluOpType

D = 1024
F = 2816
NFC = 22
L = 4
S = 8192
TOK = 4096
G = 512
NG = TOK // G
EPS = 1e-6
NEG = -30000.0
NV = 40
SLOT = 3584
NSLOT = 6

C_NAQ, C_NAK, C_NAV, C_DFQ, C_DFK, C_DFV, C_GQQ, C_GQK, C_GQV, C_DLQ, C_DLK, C_DLV = (
    0, 256, 512, 768, 1024, 1280, 1536, 1792, 1920, 2048, 2304, 2560)
QK_CHUNKS = [(C_NAQ, "na", 0), (C_NAQ + 128, "na", 0), (C_DFQ, "diff", 0), (C_DFQ + 128, "diff", 0),
             (C_GQQ, "gqa", 0), (C_GQQ + 128, "gqa", 0), (C_DLQ, "dil", 0), (C_DLQ + 128, "dil", 0),
             (C_NAK, "na", 1), (C_NAK + 128, "na", 1), (C_DFK, "diff", 1), (C_DFK + 128, "diff", 1),
             (C_GQK, "gqa", 1), (C_DLK, "dil", 1), (C_DLK + 128, "dil", 1)]
HD = {"na": 64, "diff": 32, "gqa": 64, "dil": 64}
GCOL = {("na", 0): (24, None), ("na", 1): (25, None), ("diff", 0): (26, 27), ("diff", 1): (28, 29),
        ("gqa", 0): (30, 31), ("gqa", 1): (32, 33), ("dil", 0): (34, 35), ("dil", 1): (36, 37)}
TABI = {"diff": 0, "gqa": 2, "dil": 4}
V_COLS = [(C_NAV, 256), (C_DFV, 256), (C_GQV, 128), (C_DLV, 256)]


def swap_index(typ):
    d = HD[typ]
    idx = np.arange(d)
    if typ == "gqa":
        jj = idx % 32
        return (idx // 32) * 32 + (jj + 16) % 32
    if typ == "diff":
        out = idx.copy()
        out[:8] = (idx[:8] + 4) % 8
        return out
    if typ == "dil":
        out = idx.copy()
        out[:16] = (idx[:16] + 8) % 16
        return out
    return idx


def layer_tile_table():
    tab = {}
    off = 0

    def add(name, i, n):
        nonlocal off
        tab[(name, i)] = (off, n)
        off += n
    for ff in ("f1", "f2"):
        for fc in range(NFC):
            add(ff + "gu", fc, 2048)
        for c in range(8):
            add(ff + "d", c, 2816)
    for ci, (_, typ, _) in enumerate(QK_CHUNKS):
        add("qk", ci, 1024 if typ == "na" else 2048)
    for i in range(2):
        add("v", i, 3584)
    for i in range(4):
        add("wo", i, 2048)
    return tab, off


TILE_TAB, LAYER_W = layer_tile_table()


def lhsT_tile(w, c0, ncols=128):
    return w[:, c0:c0 + ncols].reshape(8, 128, ncols).transpose(1, 0, 2)


def prep_weights(inp):
    wall = np.empty((128, L * LAYER_W), np.float32)
    for l in range(L):
        base = l * LAYER_W
        for ff, (ng, nu, nd) in (("f1", ("ffn1_w_gate", "ffn1_w_up", "ffn1_w_down")),
                                 ("f2", ("ffn2_w_gate", "ffn2_w_up", "ffn2_w_down"))):
            wg, wu, wd = inp[ng][l], inp[nu][l], inp[nd][l]
            for fc in range(NFC):
                o, n = TILE_TAB[(ff + "gu", fc)]
                t = np.stack([lhsT_tile(wg, fc * 128), lhsT_tile(wu, fc * 128)], axis=1)
                wall[:, base + o:base + o + n] = t.reshape(128, n)
            wd3 = wd.reshape(NFC, 128, D)
            for c in range(8):
                o, n = TILE_TAB[(ff + "d", c)]
                t = wd3[:, :, c * 128:(c + 1) * 128].transpose(1, 0, 2)
                wall[:, base + o:base + o + n] = t.reshape(128, n)
        w_in = inp["w_in"][l]
        for ci, (c0, typ, _) in enumerate(QK_CHUNKS):
            o, n = TILE_TAB[("qk", ci)]
            main = lhsT_tile(w_in, c0)
            if typ == "na":
                wall[:, base + o:base + o + n] = main.reshape(128, n)
            else:
                d = HD[typ]
                sw = swap_index(typ)
                cols = c0 + (np.arange(128) // d) * d + sw[np.arange(128) % d]
                swp = w_in[:, cols].reshape(8, 128, 128).transpose(1, 0, 2)
                wall[:, base + o:base + o + n] = np.stack([main, swp], axis=1).reshape(128, n)
        wv = np.concatenate([w_in[:, c0:c0 + n_] for c0, n_ in V_COLS], axis=1)
        wv3 = wv.reshape(8, 128, 896).transpose(1, 0, 2)
        for i in range(2):
            o, n = TILE_TAB[("v", i)]
            wall[:, base + o:base + o + n] = wv3[:, :, i * 448:(i + 1) * 448].reshape(128, n)
        wo = inp["w_out"][l]
        for i in range(4):
            o, n = TILE_TAB[("wo", i)]
            t = np.stack([lhsT_tile(wo, (2 * i) * 128), lhsT_tile(wo, (2 * i + 1) * 128)], axis=1)
            wall[:, base + o:base + o + n] = t.reshape(128, n)
    return wall


def prep_vecs(inp):
    vecs = np.zeros((128, L * NV), np.float32)
    p = np.arange(128)
    for l in range(L):
        b = l * NV
        vecs[:, b + 0:b + 8] = inp["ffn1_norm"][l].reshape(8, 128).T
        vecs[:, b + 8:b + 16] = inp["mix_norm"][l].reshape(8, 128).T
        vecs[:, b + 16:b + 24] = inp["ffn2_norm"][l].reshape(8, 128).T
        names = {("na", 0): "na_q_norm", ("na", 1): "na_k_norm", ("diff", 0): "diff_q_norm",
                 ("diff", 1): "diff_k_norm", ("gqa", 0): "gqa_q_norm", ("gqa", 1): "gqa_k_norm",
                 ("dil", 0): "dil_q_norm", ("dil", 1): "dil_k_norm"}
        for key, nm in names.items():
            g = inp[nm][l]
            d = HD[key[0]]
            c_main, c_sw = GCOL[key]
            vecs[:, b + c_main] = g[p % d]
            if c_sw is not None:
                vecs[:, b + c_sw] = g[swap_index(key[0])[p % d]]
        vecs[:, b + 38] = inp["diff_out_norm"][l][p % 64]
    lam = np.zeros((64, L * 128), np.float32)
    for l in range(L):
        for i, nm in enumerate(("diff_lambda_q1", "diff_lambda_k1", "diff_lambda_q2", "diff_lambda_k2")):
            lam[:, l * 128 + i * 32:l * 128 + (i + 1) * 32] = np.broadcast_to(inp[nm][l][None, :], (64, 32))
    rb = inp["na_rel_bias"]
    kc = np.arange(64)[:, None]
    qc = np.arange(64)[None, :]
    co = np.clip(kc - qc + 15, 0, 30)
    natab = np.zeros((L, 4, 23, 64, 64), np.float32)
    for a in range(23):
        dr = 18 - a
        if 0 <= dr <= 14:
            natab[:, :, a] = rb[:, :, dr][:, :, co]
    return vecs, lam, natab


def rope_tables_core(half):
    pos = (half * TOK + np.arange(TOK)).astype(np.int32)
    out = np.zeros((128, 6, TOK), np.float32)
    p = np.arange(128)

    def tables(posv, dim, theta):
        inv = (1.0 / (np.float32(theta) ** (np.arange(0, dim, 2, dtype=np.float32) / np.float32(dim)))).astype(np.float32)
        ang = posv.astype(np.float32)[:, None] * inv[None, :]
        return np.cos(ang).astype(np.float32), np.sin(ang).astype(np.float32)
    c, s = tables(pos, 8, 500000.0)
    j = p % 32
    for pp in range(128):
        jj = j[pp]
        if jj < 8:
            out[pp, 0] = c[:, jj % 4]
            out[pp, 1] = (-s[:, jj % 4]) if jj < 4 else s[:, jj % 4]
        else:
            out[pp, 0] = 1.0
    cr, sr = tables(pos // 64, 32, 10000.0)
    cc, sc = tables(pos % 64, 32, 10000.0)
    j = p % 64
    for pp in range(128):
        jj = j[pp] % 32
        cT, sT = (cr, sr) if j[pp] < 32 else (cc, sc)
        out[pp, 2] = cT[:, jj % 16]
        out[pp, 3] = (-sT[:, jj % 16]) if jj < 16 else sT[:, jj % 16]
    c, s = tables(pos, 16, 500000.0)
    for pp in range(128):
        jj = j[pp]
        if jj < 16:
            out[pp, 4] = c[:, jj % 8]
            out[pp, 5] = (-s[:, jj % 8]) if jj < 8 else s[:, jj % 8]
        else:
            out[pp, 4] = 1.0
    return np.ascontiguousarray(out.reshape(128, 6, NG, G).transpose(2, 0, 1, 3))


def na_rowmask(r0):
    out = np.full((8, 2, 64, 8, 64), NEG, np.float32)
    kc = np.arange(64)[:, None]
    qc = np.arange(64)[None, :]
    cs = np.clip(qc - 8, 0, 48)
    colok = (kc >= cs) & (kc < cs + 16)
    for j in range(8):
        for krl in range(2):
            kr = r0 - 4 + 2 * j + krl
            if kr < 0 or kr > 127:
                continue
            for qrl in range(8):
                qr = r0 + qrl
                rs = min(max(qr - 4, 0), 120)
                if rs <= kr < rs + 8:
                    out[j, krl, :, qrl, :] = np.where(colok, 0.0, NEG)
    return out.reshape(8, 128, 512)


def dil_bias():
    p = np.arange(128)[:, None]
    q = np.arange(512)[None, :]
    out = np.zeros((20, 128, 512), np.float32)
    for i in range(20):
        dl = 128 * i - 1024 + p - q
        a = np.abs(dl)
        c = (a <= 64).astype(np.int32) + ((dl % 4 == 0) & (a <= 256)) + ((dl % 16 == 0) & (a <= 1024))
        out[i] = np.where(c > 0, np.log(np.maximum(c, 1).astype(np.float32)), NEG)
    return out.astype(np.float32)


def block_consts():
    bd = np.zeros((128, 3, 128), np.float32)
    bd[:, 0, :] = 1.0
    for b in range(2):
        bd[b * 64:(b + 1) * 64, 1, b * 64:(b + 1) * 64] = 1.0
    for b in range(4):
        bd[b * 32:(b + 1) * 32, 2, b * 32:(b + 1) * 32] = 1.0
    return bd.reshape(128, 384)


class Buf:
    __slots__ = ("name", "t", "w", "r")

    def __init__(self, name, t=None):
        self.name = name
        self.t = t
        self.w = []
        self.r = []

    def __getitem__(self, idx):
        return self.t[idx]


class Trk:
    def __init__(self, nc, n_dma_sems=40):
        self.nc = nc
        self.eng = {"pe": nc.tensor, "act": nc.scalar, "dve": nc.vector, "pool": nc.gpsimd, "sp": nc.sync}
        self.sem, self.cnt = {}, {}
        for k in ("pe", "act", "dve", "pool"):
            self.sem[k] = nc.alloc_semaphore("s_" + k)
            self.cnt[k] = 0
        self.nd = n_dma_sems
        for i in range(n_dma_sems):
            self.sem["d%d" % i] = nc.alloc_semaphore("d%d" % i)
            self.cnt["d%d" % i] = 0
        self.dnext = 0
        self.known = {e: {} for e in self.eng}

    def _need(self, e, key, count):
        if key == e and e == "pe":
            return
        kn = self.known[e]
        if kn.get(key, 0) >= count:
            return
        self.eng[e].wait_ge(self.sem[key], count)
        kn[key] = count

    @staticmethod
    def _compress(lst):
        mx = {}
        for k_, c_ in lst:
            if mx.get(k_, 0) < c_:
                mx[k_] = c_
        return list(mx.items())

    def _deps(self, e, reads, writes, acc=False):
        for b in reads:
            for ww in b.w:
                self._need(e, *ww)
        if acc:
            return
        for b in writes:
            for ww in b.w:
                self._need(e, *ww)
            for rr in b.r:
                self._need(e, *rr)

    def _mark(self, ev, reads, writes, acc=False):
        for b in reads:
            b.r.append(ev)
            if len(b.r) > 48:
                b.r = self._compress(b.r)
        for b in writes:
            if acc:
                b.w.append(ev)
                if len(b.w) > 48:
                    b.w = self._compress(b.w)
            else:
                b.w = [ev]
                b.r = []

    def op(self, e, fn, reads=(), writes=()):
        self._deps(e, reads, writes)
        ins = fn(self.eng[e])
        self.cnt[e] += 1
        ins.then_inc(self.sem[e], 1)
        ev = (e, self.cnt[e])
        self._mark(ev, reads, writes)
        return ev

    def dma(self, e, out, in_, reads=(), writes=(), acc=False, **kw):
        self._deps(e, reads, writes, acc)
        key = "d%d" % self.dnext
        self.dnext = (self.dnext + 1) % self.nd
        if self.cnt[key] > 0:
            self._need(e, key, self.cnt[key])
        ins = self.eng[e].dma_start(out=out, in_=in_, **kw)
        self.cnt[key] += 16
        ins.then_inc(self.sem[key], 16)
        ev = (key, self.cnt[key])
        self._mark(ev, reads, writes, acc)
        return ev

    def wait_all(self, e, bufs):
        for b in bufs:
            for ww in b.w:
                self._need(e, *ww)
            for rr in b.r:
                self._need(e, *rr)


class Ring:
    def __init__(self, bufs):
        self.bufs = bufs
        self.i = 0

    def next(self):
        b = self.bufs[self.i]
        self.i = (self.i + 1) % len(self.bufs)
        return b


class Prog:
    def __init__(self, stages, exchange="host"):
        self.stages = stages
        self.exchange = exchange
        nc = self.nc = bass.Bass("TRN2", target_bir_lowering=False)
        self.k = Trk(nc)
        self.dram = {}
        self.ext_in, self.ext_out = [], []
        sset = set(stages)
        self.layers_w = sorted({l for (_, l) in stages if _ in ("TA", "TB")})

        def dt(name, shape, dtype, kind):
            t = nc.dram_tensor(name, shape, dtype, kind=kind)
            self.dram[name] = Buf(name, t.ap())
            if kind == "ExternalInput":
                self.ext_in.append(name)
            if kind == "ExternalOutput":
                self.ext_out.append(name)
            return self.dram[name]
        first = stages[0]
        last = stages[-1]
        if self.layers_w:
            self.x_in = dt("x_in", [8, 128, TOK], F32, "ExternalInput")
            self.x_out = dt("x_out", [8, 128, TOK], F32, "ExternalOutput")
        if self.layers_w:
            self.wall = dt("wall", [128, len(self.layers_w) * LAYER_W], F32, "ExternalInput")
            self.ws = dt("ws", [128, len(self.layers_w) * LAYER_W], BF16, "Internal")
        self.nsl = max(l_ for (_, l_) in stages) + 1
        self.vecs = dt("vecs", [128, self.nsl * NV], F32, "ExternalInput")
        self.bdc = dt("bdc", [128, 384], F32, "ExternalInput")
        if any(s_[0] == "TA" for s_ in stages):
            self.tabs = dt("tabs", [NG, 128, 6 * G], F32, "ExternalInput")
        if any(s_[0] == "ATT" for s_ in stages):
            self.lam = dt("lam", [64, self.nsl * 130], F32, "ExternalInput")
            self.natab = dt("natab", [self.nsl * 4 * 23 * 64, 64], F32, "ExternalInput")
            self.rmc = dt("rmc", [24, 128, 512], F32, "ExternalInput")
            self.dilb = dt("dilb", [20, 128, 512], F32, "ExternalInput")
            self.vpn = dt("vpn", [128, 2], F32, "ExternalInput")
        self.qt, self.kto, self.vo, self.ktg, self.vg, self.mt = {}, {}, {}, {}, {}, {}
        for l in range(self.nsl):
            ta, att, tb = ("TA", l) in sset, ("ATT", l) in sset, ("TB", l) in sset
            if ta or att:
                kind_o = "Internal" if (ta and att) else ("ExternalOutput" if ta else "ExternalInput")
                self.qt[l] = dt("qt%d" % l, [8, 128, TOK], BF16, kind_o)
                if ta:
                    k2 = "Internal" if (att and exchange == "cc") else "ExternalOutput"
                    self.kto[l] = dt("kto%d" % l, [7 * 128, TOK], BF16, k2)
                    self.vo[l] = dt("vo%d" % l, [TOK, 896], BF16, k2)
                else:
                    self.kto[l] = dt("kto%d" % l, [7 * 128, TOK], BF16, "ExternalInput")
                    self.vo[l] = dt("vo%d" % l, [TOK, 896], BF16, "ExternalInput")
                if att:
                    k3 = "Internal" if (ta and exchange == "cc") else "ExternalInput"
                    self.ktg[l] = dt("ktg%d" % l, [2 * 7 * 128, TOK], BF16, k3)
                    self.vg[l] = dt("vg%d" % l, [2 * TOK, 896], BF16, k3)
            if att or tb:
                kind_m = "Internal" if (att and tb) else ("ExternalOutput" if att else "ExternalInput")
                self.mt[l] = dt("mt%d" % l, [8, 128, TOK], BF16, kind_m)
        if exchange == "cc":
            self.g8k = dt("g8k", [8 * 896, TOK], BF16, "Internal")
            self.g8v = dt("g8v", [8 * TOK, 896], BF16, "Internal")
        self.build()

    def exchange_kv(self, l):
        nc, k = self.nc, self.k
        g = nc.gpsimd
        if not hasattr(self, "cc_sem"):
            self.cc_sem = nc.alloc_semaphore("cc")
            self.cc_cnt = 0
            self.pid = g.partition_id()
        for src, g8, dst, rows in ((self.kto[l], self.g8k, self.ktg[l], 896), (self.vo[l], self.g8v, self.vg[l], TOK)):
            k._deps("pool", [src], [g8])
            g.collective_compute("AllGather", ALU.bypass, replica_groups=[list(range(8))],
                                 ins=[src[:, :]], outs=[g8[:, :]]).then_inc(self.cc_sem, 1)
            self.cc_cnt += 1
            g.wait_ge(self.cc_sem, self.cc_cnt)
            src.r.append(("pool", k.cnt["pool"]))
            g8.w = []
            g8.r = []
            base = (self.pid // 2) * (2 * rows)
            k.dma("pool", dst[:, :], g8[bass.ds(base, 2 * rows), :], reads=[g8], writes=[dst])

    def _nm(self, base):
        self._uid = getattr(self, "_uid", 0) + 1
        return "%s_%d" % (base, self._uid)

    def sb(self, name, shape, dtype):
        return Buf(name, self.nc.alloc_sbuf_tensor(name, shape, dtype))

    def build(self):
        nc, k = self.nc, self.k
        self.vec_sb = self.sb("vec_sb", [128, self.nsl * NV], F32)
        k.dma("sp", self.vec_sb[:], self.vecs[:, :], writes=[self.vec_sb])
        bdf = self.sb("bdf", [128, 384], F32)
        self.bd = self.sb("bd", [128, 384], BF16)
        k.dma("sp", bdf[:], self.bdc[:, :], writes=[bdf])
        k.op("dve", lambda e: e.tensor_copy(out=self.bd[:], in_=bdf[:]), reads=[bdf], writes=[self.bd])
        self.psum = Ring([Buf("ps%d" % i, nc.alloc_psum_tensor("ps%d" % i, [128, 512], F32)) for i in range(8)])
        self.cast_weights()
        i = 0
        st = self.stages
        while i < len(st):
            if st[i][0] in ("TA", "TB"):
                j = i
                while j < len(st) and st[j][0] in ("TA", "TB"):
                    j += 1
                self.t_phase(st[i:j])
                i = j
            else:
                if self.exchange == "cc":
                    self.exchange_kv(st[i][1])
                self.att_phase(st[i][1])
                i += 1
        k.wait_all("sp", [self.dram[n] for n in self.ext_out])

    def cast_weights(self):
        nc, k = self.nc, self.k
        tot = len(self.layers_w) * LAYER_W
        if tot == 0:
            return
        CH = 4096
        with nc.sbuf_tensor(self._nm("cst_f"), [128, 3, CH], F32) as cf_t, nc.sbuf_tensor(self._nm("cst_b"), [128, 3, CH], BF16) as cb_t:
            cf = [Buf("cf%d" % i, cf_t[:, i, :]) for i in range(3)]
            cb = [Buf("cb%d" % i, cb_t[:, i, :]) for i in range(3)]
            engs = ["dve", "pool", "act"]
            i = 0
            for c0 in range(0, tot, CH):
                n = min(CH, tot - c0)
                f, b = cf[i % 3], cb[i % 3]
                k.dma("sp", f.t[:, 0:n], self.wall[:, c0:c0 + n], writes=[f])
                e = engs[i % 3]
                if e == "act":
                    k.op(e, lambda en, f=f, b=b, n=n: en.copy(out=b.t[:, 0:n], in_=f.t[:, 0:n]), reads=[f], writes=[b])
                else:
                    k.op(e, lambda en, f=f, b=b, n=n: en.tensor_copy(out=b.t[:, 0:n], in_=f.t[:, 0:n]), reads=[f], writes=[b])
                k.dma("pool", self.ws[:, c0:c0 + n], b.t[:, 0:n], reads=[b], writes=[self.ws], acc=True)
                i += 1
            k.wait_all("sp", [self.ws])
            k.wait_all("pool", cf + cb)
            k.wait_all("dve", cf + cb)
            k.wait_all("act", cf + cb)
            k.wait_all("sp", cf + cb)

    def t_phase(self, tst):
        nc, k = self.nc, self.k
        vec = self.vec_sb
        with (nc.sbuf_tensor(self._nm("t_x"), [128, 2, 8, G], F32) as x_t,
              nc.sbuf_tensor(self._nm("t_m"), [128, 2, 8, G], BF16) as m_t,
              nc.sbuf_tensor(self._nm("t_h"), [128, 8, G], BF16) as h_t,
              nc.sbuf_tensor(self._nm("t_act"), [128, NFC, G], BF16) as act_t,
              nc.sbuf_tensor(self._nm("t_sq"), [128, 3, G], BF16) as sq_t,
              nc.sbuf_tensor(self._nm("t_f32"), [128, 8, G], F32) as f32_t,
              nc.sbuf_tensor(self._nm("t_qo"), [128, 3, G], BF16) as qo_t,
              nc.sbuf_tensor(self._nm("t_vo"), [128, 2, 896], BF16) as vo_t,
              nc.sbuf_tensor(self._nm("t_tab"), [128, 6, G], F32) as tab_t,
              nc.sbuf_tensor(self._nm("t_w"), [128, NSLOT, SLOT], BF16) as w_t):
            xb = [[Buf("x%d_%d" % (i, c), x_t[:, i, c, :]) for c in range(8)] for i in range(2)]
            mbb = [[Buf("m%d_%d" % (i, c), m_t[:, i, c, :]) for c in range(8)] for i in range(2)]
            hb = [Buf("h%d" % c, h_t[:, c, :]) for c in range(8)]
            actb = [Buf("act%d" % i, act_t[:, i, :]) for i in range(NFC)]
            sqr = Ring([Buf("sq%d" % i, sq_t[:, i, :]) for i in range(3)])
            f32r = Ring([Buf("f32_%d" % i, f32_t[:, i, :]) for i in range(8)])
            qor = Ring([Buf("qo%d" % i, qo_t[:, i, :]) for i in range(3)])
            vor = Ring([Buf("vo%d" % i, vo_t[:, i, :]) for i in range(2)])
            tabb = Buf("tab", tab_t[:])
            wslots = Ring([Buf("w%d" % i, w_t[:, i, :]) for i in range(NSLOT)])

            def stage_tiles(kind, l):
                lp = self.layers_w.index(l)
                seq = []
                if kind == "TB":
                    seq += [(lp, "wo", i) for i in range(4)]
                    seq += [(lp, "f2gu", i) for i in range(NFC)] + [(lp, "f2d", i) for i in range(8)]
                else:
                    seq += [(lp, "f1gu", i) for i in range(NFC)] + [(lp, "f1d", i) for i in range(8)]
                    seq += [(lp, "qk", i) for i in range(len(QK_CHUNKS))] + [(lp, "v", i) for i in range(2)]
                return seq
            wseq = []
            for g in range(NG):
                for (kind, l) in tst:
                    wseq += stage_tiles(kind, l)
            wstate = {"issued": 0, "taken": 0, "q": []}

            def w_issue():
                lp, name, idx = wseq[wstate["issued"]]
                o, n = TILE_TAB[(name, idx)]
                slot = wslots.next()
                k.dma("sp", slot.t[:, 0:n], self.ws[:, lp * LAYER_W + o:lp * LAYER_W + o + n],
                      reads=[self.ws], writes=[slot])
                wstate["q"].append((slot, name, idx))
                wstate["issued"] += 1

            def w_get(name, idx):
                while wstate["issued"] < len(wseq) and wstate["issued"] - wstate["taken"] < NSLOT - 1:
                    w_issue()
                slot, nm, ix = wstate["q"].pop(0)
                assert (nm, ix) == (name, idx), (nm, ix, name, idx)
                wstate["taken"] += 1
                return slot

            has_tb = tst[0][0] == "TB"
            l_tb = tst[0][1] if has_tb else None
            ta_list = [s for s in tst if s[0] == "TA"]
            l_ta = ta_list[0][1] if ta_list else None
            if has_tb:
                x_src = self.dram["x_in"] if ("TA", l_tb) not in set(self.stages) else self.x1
            else:
                x_src = self.dram["x_in"]
            if l_ta is not None:
                x_dst = self.dram["x_out"] if ("TB", l_ta) not in set(self.stages) else self._x1_tensor()
            else:
                x_dst = self.dram["x_out"]

            def load_group(g):
                xg = xb[g % 2]
                k.dma("pool", x_t[:, g % 2], x_src[:, :, g * G:(g + 1) * G].rearrange("c p t -> p c t"),
                      reads=[x_src], writes=xg)
                if has_tb:
                    k.dma("pool", m_t[:, g % 2], self.mt[l_tb][:, :, g * G:(g + 1) * G].rearrange("c p t -> p c t"),
                          reads=[self.mt[l_tb]], writes=mbb[g % 2])

            def norm(xg, gcol0):
                ss = self.psum.next()
                for c in range(8):
                    sq = sqr.next()
                    k.op("pool", lambda e, sq=sq, c=c: e.tensor_tensor(out=sq.t, in0=xg[c].t, in1=xg[c].t, op=ALU.mult),
                         reads=[xg[c]], writes=[sq])
                    k.op("pe", lambda e, sq=sq, c=c: e.matmul(ss.t[:, :], lhsT=self.bd[:, 0:128], rhs=sq.t, start=(c == 0), stop=(c == 7)),
                         reads=[sq, self.bd], writes=[ss])
                rs = f32r.next()
                k.op("act", lambda e: e.activation(out=rs.t, in_=ss.t[:, :], func=AF.Sqrt, scale=1.0 / D, bias=EPS),
                     reads=[ss], writes=[rs])
                k.op("dve", lambda e: e.reciprocal(out=rs.t, in_=rs.t), reads=[rs], writes=[rs])
                for c in range(8):
                    k.op("dve", lambda e, c=c: e.scalar_tensor_tensor(out=hb[c].t, in0=xg[c].t,
                                                                      scalar=vec[:, gcol0 + c:gcol0 + c + 1], in1=rs.t,
                                                                      op0=ALU.mult, op1=ALU.mult),
                         reads=[xg[c], rs, vec], writes=[hb[c]])

            def ffn(xg, ff):
                for fc in range(NFC):
                    w = w_get(ff + "gu", fc)
                    wv_ = w.t[:, 0:2048].rearrange("p (a k f) -> p a k f", a=2, k=8)
                    gp, up = self.psum.next(), self.psum.next()
                    for a, pt in ((0, gp), (1, up)):
                        for kk in range(8):
                            k.op("pe", lambda e, a=a, pt=pt, kk=kk: e.matmul(pt.t[:, :], lhsT=wv_[:, a, kk, :], rhs=hb[kk].t,
                                                                            start=(kk == 0), stop=(kk == 7)),
                                 reads=[w, hb[kk]], writes=[pt])
                    sg = f32r.next()
                    k.op("act", lambda e: e.activation(out=sg.t, in_=gp.t[:, :], func=AF.Silu), reads=[gp], writes=[sg])
                    k.op("dve", lambda e, fc=fc: e.tensor_tensor(out=actb[fc].t, in0=sg.t, in1=up.t[:, :], op=ALU.mult),
                         reads=[sg, up], writes=[actb[fc]])
                for c in range(8):
                    w = w_get(ff + "d", c)
                    wv_ = w.t[:, 0:2816].rearrange("p (k f) -> p k f", k=NFC)
                    yp = self.psum.next()
                    for fk in range(NFC):
                        k.op("pe", lambda e, fk=fk: e.matmul(yp.t[:, :], lhsT=wv_[:, fk, :], rhs=actb[fk].t,
                                                             start=(fk == 0), stop=(fk == NFC - 1)),
                             reads=[w, actb[fk]], writes=[yp])
                    k.op("dve", lambda e, c=c: e.scalar_tensor_tensor(out=xg[c].t, in0=yp.t[:, :], scalar=0.5,
                                                                      in1=xg[c].t, op0=ALU.mult, op1=ALU.add),
                         reads=[yp, xg[c]], writes=[xg[c]])

            def out_proj(xg, mb):
                for i in range(4):
                    w = w_get("wo", i)
                    wv_ = w.t[:, 0:2048].rearrange("p (a k f) -> p a k f", a=2, k=8)
                    for a in range(2):
                        c = 2 * i + a
                        yp = self.psum.next()
                        for kk in range(8):
                            k.op("pe", lambda e, a=a, kk=kk: e.matmul(yp.t[:, :], lhsT=wv_[:, a, kk, :], rhs=mb[kk].t,
                                                                      start=(kk == 0), stop=(kk == 7)),
                                 reads=[w, mb[kk]], writes=[yp])
                        k.op("dve", lambda e, c=c: e.tensor_tensor(out=xg[c].t, in0=yp.t[:, :], in1=xg[c].t, op=ALU.add),
                             reads=[yp, xg[c]], writes=[xg[c]])

            def in_proj(g, l):
                vb = l * NV
                k.dma("pool", tabb.t, self.tabs[g].rearrange("p (a t) -> p a t", a=6), reads=[self.tabs], writes=[tabb])
                qt_d, kt_d, v_d = self.qt[l], self.kto[l], self.vo[l]
                for ci, (_, typ, is_k) in enumerate(QK_CHUNKS):
                    w = w_get("qk", ci)
                    roped = typ != "na"
                    d = HD[typ]
                    if roped:
                        wv_ = w.t[:, 0:2048].rearrange("p (a k f) -> p a k f", a=2, k=8)
                    else:
                        wv_ = w.t[:, 0:1024].rearrange("p (a k f) -> p a k f", a=1, k=8)
                    pm = self.psum.next()
                    psw = self.psum.next() if roped else None
                    for a, pt in ((0, pm), (1, psw)):
                        if pt is None:
                            continue
                        for kk in range(8):
                            k.op("pe", lambda e, a=a, pt=pt, kk=kk: e.matmul(pt.t[:, :], lhsT=wv_[:, a, kk, :], rhs=hb[kk].t,
                                                                            start=(kk == 0), stop=(kk == 7)),
                                 reads=[w, hb[kk]], writes=[pt])
                    sq = sqr.next()
                    k.op("act", lambda e: e.activation(out=sq.t, in_=pm.t[:, :], func=AF.Square), reads=[pm], writes=[sq])
                    ss = self.psum.next()
                    bsel = 1 if d == 64 else 2
                    k.op("pe", lambda e: e.matmul(ss.t[:, :], lhsT=self.bd[:, bsel * 128:(bsel + 1) * 128], rhs=sq.t, start=True, stop=True),
                         reads=[sq, self.bd], writes=[ss])
                    rs = f32r.next()
                    k.op("act", lambda e: e.activation(out=rs.t, in_=ss.t[:, :], func=AF.Sqrt, scale=1.0 / d, bias=EPS),
                         reads=[ss], writes=[rs])
                    k.op("dve", lambda e: e.reciprocal(out=rs.t, in_=rs.t), reads=[rs], writes=[rs])
                    gm, gs = GCOL[(typ, is_k)]
                    qo = qor.next()
                    if not roped:
                        k.op("dve", lambda e: e.scalar_tensor_tensor(out=qo.t, in0=pm.t[:, :], scalar=vec[:, vb + gm:vb + gm + 1],
                                                                     in1=rs.t, op0=ALU.mult, op1=ALU.mult),
                             reads=[pm, rs, vec], writes=[qo])
                    else:
                        ti = TABI[typ]
                        t1, t2 = f32r.next(), f32r.next()
                        k.op("dve", lambda e: e.scalar_tensor_tensor(out=t1.t, in0=pm.t[:, :], scalar=vec[:, vb + gm:vb + gm + 1],
                                                                     in1=tabb.t[:, ti, :], op0=ALU.mult, op1=ALU.mult),
                             reads=[pm, tabb, vec], writes=[t1])
                        k.op("dve", lambda e: e.scalar_tensor_tensor(out=t2.t, in0=psw.t[:, :], scalar=vec[:, vb + gs:vb + gs + 1],
                                                                     in1=tabb.t[:, ti + 1, :], op0=ALU.mult, op1=ALU.mult),
                             reads=[psw, tabb, vec], writes=[t2])
                        k.op("pool", lambda e: e.tensor_tensor(out=t1.t, in0=t1.t, in1=t2.t, op=ALU.add), reads=[t1, t2], writes=[t1])
                        k.op("pool", lambda e: e.tensor_tensor(out=qo.t, in0=t1.t, in1=rs.t, op=ALU.mult), reads=[t1, rs], writes=[qo])
                    if not is_k:
                        k.dma("pool", qt_d[ci, :, g * G:(g + 1) * G], qo.t, reads=[qo], writes=[qt_d], acc=True)
                    else:
                        r0 = (ci - 8) * 128
                        k.dma("pool", kt_d[r0:r0 + 128, g * G:(g + 1) * G], qo.t, reads=[qo], writes=[kt_d], acc=True)
                wv0, wv1 = w_get("v", 0), w_get("v", 1)
                wvs = [wv0.t[:, 0:3584].rearrange("p (k f) -> p k f", k=8), wv1.t[:, 0:3584].rearrange("p (k f) -> p k f", k=8)]
                for tt in range(4):
                    vo_b = vor.next()
                    for hf in range(2):
                        vp_ = self.psum.next()
                        for kk in range(8):
                            k.op("pe", lambda e, hf=hf, kk=kk, tt=tt: e.matmul(vp_.t[:, 0:448], lhsT=hb[kk].t[:, tt * 128:(tt + 1) * 128],
                                                                               rhs=wvs[hf][:, kk, :], start=(kk == 0), stop=(kk == 7)),
                                 reads=[(wv0, wv1)[hf], hb[kk]], writes=[vp_])
                        k.op("act", lambda e, hf=hf: e.copy(out=vo_b.t[:, hf * 448:(hf + 1) * 448], in_=vp_.t[:, 0:448]),
                             reads=[vp_], writes=[vo_b])
                    t0 = g * G + tt * 128
                    k.dma("pool", v_d[t0:t0 + 128, :], vo_b.t, reads=[vo_b], writes=[v_d], acc=True)

            load_group(0)
            for g in range(NG):
                xg = xb[g % 2]
                if g + 1 < NG:
                    load_group(g + 1)
                for (kind, l) in tst:
                    vb = l * NV
                    if kind == "TB":
                        out_proj(xg, mbb[g % 2])
                        norm(xg, vb + 16)
                        ffn(xg, "f2")
                    else:
                        norm(xg, vb + 0)
                        ffn(xg, "f1")
                        norm(xg, vb + 8)
                        in_proj(g, l)
                k.dma("pool", x_dst[:, :, g * G:(g + 1) * G].rearrange("c p t -> p c t"), x_t[:, g % 2], reads=xg, writes=[x_dst])
            allb = xb[0] + xb[1] + mbb[0] + mbb[1] + hb + actb + sqr.bufs + f32r.bufs + qor.bufs + vor.bufs + [tabb] + wslots.bufs
            for e in ("sp", "pool", "act", "dve", "pe"):
                k.wait_all(e, allb)

    def _x1_tensor(self):
        if not hasattr(self, "x1"):
            t = self.nc.dram_tensor("x1s", [8, 128, TOK], F32, kind="Internal")
            self.x1 = Buf("x1s", t.ap())
        return self.x1

    def att_phase(self, l):
        nc, k = self.nc, self.k
        vec = self.vec_sb
        vb = l * NV
        ktg, vg, kto, vo, qt, mt = self.ktg[l], self.vg[l], self.kto[l], self.vo[l], self.qt[l], self.mt[l]
        ps = self.psum.bufs
        s_ring, o_ring = Ring(ps[0:4]), Ring(ps[4:8])
        m_ring = s_ring
        with (nc.sbuf_tensor(self._nm("a_kt"), [128, 8192], BF16) as kt_t,
              nc.sbuf_tensor(self._nm("a_va"), [128, 64, 128], BF16) as va_t,
              nc.sbuf_tensor(self._nm("a_q"), [128, TOK], BF16) as q_t,
              nc.sbuf_tensor(self._nm("a_p"), [128, 8, 512], BF16) as p_t,
              nc.sbuf_tensor(self._nm("a_s"), [128, 6, 512], F32) as s_t,
              nc.sbuf_tensor(self._nm("a_bias"), [128, 32, 512], F32) as b_t,
              nc.sbuf_tensor(self._nm("a_post"), [128, 8, 512], F32) as post_t,
              nc.sbuf_tensor(self._nm("a_ob"), [64, 3, 512], BF16) as ob_t,
              nc.sbuf_tensor(self._nm("a_small"), [64, 160], F32) as sm_t,
              nc.sbuf_tensor(self._nm("a_vpn"), [128, 2], F32) as vpn_t):
            ktb = Buf("kt", kt_t[:])
            vab = Buf("va", va_t[:])
            vones = Buf("vones", None)
            qb = Buf("q", q_t[:])
            pr = Ring([Buf("p%d" % i, p_t[:, i, :]) for i in range(8)])
            sr = Ring([Buf("s%d" % i, s_t[:, i, :]) for i in range(6)])
            bias = [Buf("b%d" % i, b_t[:, i, :]) for i in range(32)]
            postr = Ring([Buf("po%d" % i, post_t[:, i, :]) for i in range(8)])
            obr = Ring([Buf("ob%d" % i, ob_t[:, i, :]) for i in range(3)])
            smb = Buf("sm", sm_t[:])
            vpnb = Buf("vpn", vpn_t[:])
            k.dma("sp", vpn_t[:], self.vpn[:, :], writes=[vpnb])
            k.dma("sp", sm_t[:, 0:130], self.lam[:, l * 130:(l + 1) * 130], writes=[smb])
            k.op("dve", lambda e: e.tensor_tensor(out=sm_t[:, 0:32], in0=sm_t[:, 0:32], in1=sm_t[:, 32:64], op=ALU.mult), reads=[smb], writes=[smb])
            k.op("dve", lambda e: e.tensor_tensor(out=sm_t[:, 64:96], in0=sm_t[:, 64:96], in1=sm_t[:, 96:128], op=ALU.mult), reads=[smb], writes=[smb])
            k.op("dve", lambda e: e.reduce_sum(out=sm_t[:, 130:131], in_=sm_t[:, 0:32], axis=mybir.AxisListType.X), reads=[smb], writes=[smb])
            k.op("dve", lambda e: e.reduce_sum(out=sm_t[:, 131:132], in_=sm_t[:, 64:96], axis=mybir.AxisListType.X), reads=[smb], writes=[smb])
            k.op("act", lambda e: e.activation(out=sm_t[:, 132:134], in_=sm_t[:, 130:132], func=AF.Exp), reads=[smb], writes=[smb])
            k.op("dve", lambda e: e.tensor_tensor(out=sm_t[:, 134:135], in0=sm_t[:, 133:134], in1=sm_t[:, 132:133], op=ALU.subtract), reads=[smb], writes=[smb])
            k.op("dve", lambda e: e.tensor_tensor(out=sm_t[:, 134:135], in0=sm_t[:, 134:135], in1=sm_t[:, 128:129], op=ALU.add), reads=[smb], writes=[smb])
            k.op("dve", lambda e: e.tensor_tensor(out=sm_t[:, 135:136], in0=vec[0:64, vb + 38:vb + 39], in1=sm_t[:, 129:130], op=ALU.mult), reads=[smb, vec], writes=[smb])
            k.op("pool", lambda e: e.memset(va_t[:, :, 64:128], 1.0), writes=[vones])
            k.op("pool", lambda e: e.tensor_scalar(out=va_t[:, 0:8, 64:128], in0=va_t[:, 0:8, 64:128], scalar1=vpn_t[:, 0:1], scalar2=1.0, op0=ALU.mult, op1=ALU.mult),
                 reads=[vpnb, vones], writes=[vones])
            k.op("pool", lambda e: e.tensor_scalar(out=va_t[:, 40:48, 64:128], in0=va_t[:, 40:48, 64:128], scalar1=vpn_t[:, 1:2], scalar2=1.0, op0=ALU.mult, op1=ALU.mult),
                 reads=[vpnb, vones], writes=[vones])

            def load_head(krow, qrow, ext, pair=False):
                qc, qp = qrow // 128, qrow % 128
                if pair:
                    k.dma("sp", q_t[:, :], qt[qc, :, :], reads=[qt], writes=[qb])
                else:
                    k.dma("sp", q_t[0:64, :], qt[qc, qp:qp + 64, :], reads=[qt], writes=[qb])
                if ext:
                    k.dma("sp", kt_t[0:64, 0:1024], ktg[krow:krow + 64, 3072:4096], reads=[ktg], writes=[ktb])
                    k.dma("sp", kt_t[0:64, 1024:5120], kto[krow:krow + 64, :], reads=[kto], writes=[ktb], acc=True)
                    k.dma("sp", kt_t[0:64, 5120:6144], ktg[896 + krow:896 + krow + 64, 0:1024], reads=[ktg], writes=[ktb], acc=True)
                    k.dma("sp", va_t[:, 0:8, 0:64], vg[3072:4096, krow:krow + 64].rearrange("(c p) d -> p c d", p=128),
                          reads=[vg], writes=[vab])
                    k.dma("sp", va_t[:, 8:40, 0:64], vo[:, krow:krow + 64].rearrange("(c p) d -> p c d", p=128),
                          reads=[vo], writes=[vab], acc=True)
                    k.dma("sp", va_t[:, 40:48, 0:64], vg[TOK:TOK + 1024, krow:krow + 64].rearrange("(c p) d -> p c d", p=128),
                          reads=[vg], writes=[vab], acc=True)
                    k.op("pool", lambda e: e.tensor_scalar(out=va_t[:, 0:8, 0:64], in0=va_t[:, 0:8, 0:64], scalar1=vpn_t[:, 0:1], scalar2=1.0, op0=ALU.mult, op1=ALU.mult),
                         reads=[vab, vpnb], writes=[vab])
                    k.op("pool", lambda e: e.tensor_scalar(out=va_t[:, 40:48, 0:64], in0=va_t[:, 40:48, 0:64], scalar1=vpn_t[:, 1:2], scalar2=1.0, op0=ALU.mult, op1=ALU.mult),
                         reads=[vab, vpnb], writes=[vab])
                else:
                    for r in range(2):
                        k.dma("sp", kt_t[0:64, r * TOK:(r + 1) * TOK], ktg[r * 896 + krow:r * 896 + krow + 64, :], reads=[ktg], writes=[ktb], acc=(r == 1))
                        if pair:
                            k.dma("sp", kt_t[64:128, r * TOK:(r + 1) * TOK], ktg[r * 896 + krow:r * 896 + krow + 64, :], reads=[ktg], writes=[ktb], acc=True)
                        k.dma("sp", va_t[:, r * 32:(r + 1) * 32, 0:64], vg[r * TOK:(r + 1) * TOK, krow:krow + 64].rearrange("(c p) d -> p c d", p=128),
                              reads=[vg], writes=[vab], acc=(r == 1))

            pending = []

            def flush_pending():
                while pending:
                    pending.pop(0)()

            def attend(qg, chunks, scale, maps, bias_fn=None, rm_fn=None):
                Os = [o_ring.next() for _ in maps]
                n = len(chunks)
                depth = 1 if len(maps) > 1 else 2
                pbufs = {}
                for i in range(n + depth):
                    if i == 3:
                        flush_pending()
                    if i < n:
                        kc = chunks[i]
                        for mi, (p0, kd) in enumerate(maps):
                            sp_ = s_ring.next()
                            k.op("pe", lambda e, kc=kc, sp_=sp_, p0=p0, kd=kd: e.matmul(
                                sp_.t[:, :], lhsT=kt_t[p0:p0 + kd, kc * 128:(kc + 1) * 128],
                                rhs=q_t[p0:p0 + kd, qg * G:(qg + 1) * G], start=True, stop=True),
                                 reads=[ktb, qb], writes=[sp_])
                            pb = pr.next()
                            if bias_fn is None:
                                k.op("act", lambda e, sp_=sp_, pb=pb: e.activation(out=pb.t, in_=sp_.t[:, :], func=AF.Exp, scale=scale),
                                     reads=[sp_], writes=[pb])
                            else:
                                bb = bias_fn(i)
                                s2 = sr.next()
                                k.op("dve", lambda e, sp_=sp_, s2=s2, bb=bb: e.scalar_tensor_tensor(out=s2.t, in0=sp_.t[:, :], scalar=scale, in1=bb.t,
                                                                                                   op0=ALU.mult, op1=ALU.add),
                                     reads=[sp_, bb], writes=[s2])
                                if rm_fn is not None:
                                    rb_ = rm_fn(i)
                                    k.op("pool", lambda e, s2=s2, rb_=rb_: e.tensor_tensor(out=s2.t, in0=s2.t, in1=rb_.t, op=ALU.add),
                                         reads=[s2, rb_], writes=[s2])
                                k.op("act", lambda e, s2=s2, pb=pb: e.activation(out=pb.t, in_=s2.t, func=AF.Exp), reads=[s2], writes=[pb])
                            pbufs[(mi, i)] = pb
                    j = i - depth
                    if j >= 0:
                        kc = chunks[j]
                        for mi in range(len(maps)):
                            k.op("pe", lambda e, kc=kc, j=j, mi=mi: e.matmul(Os[mi].t[:, :], lhsT=va_t[:, kc, :], rhs=pbufs[(mi, j)].t,
                                                                             start=(j == 0), stop=(j == n - 1)),
                                 reads=[vab, vones, pbufs[(mi, j)]], writes=[Os[mi]])
                return Os

            def normalize(O, dst_ap, dst_buf):
                rsb = postr.next()
                k.op("act", lambda e: e.copy(out=rsb.t[64:128, :], in_=O.t[64:128, :]), reads=[O], writes=[rsb])
                k.op("dve", lambda e: e.reciprocal(out=rsb.t[64:128, :], in_=rsb.t[64:128, :]), reads=[rsb], writes=[rsb])
                k.op("dve", lambda e: e.tensor_tensor(out=dst_ap, in0=O.t[0:64, :], in1=rsb.t[64:128, :], op=ALU.mult),
                     reads=[O, rsb], writes=[dst_buf])

            def store(ob, orow, qg):
                oc, op_ = orow // 128, orow % 128
                k.dma("pool", mt[oc, op_:op_ + 64, qg * G:(qg + 1) * G], ob.t, reads=[ob], writes=[mt], acc=True)

            def simple_head(orow, chunks_fn, scale, bias_fn=None, rm_fn=None):
                for qg in range(NG):
                    O = attend(qg, chunks_fn(qg), scale, [(0, 64)],
                               (lambda i, qg=qg: bias_fn(qg, i)) if bias_fn else None,
                               (lambda i, qg=qg: rm_fn(qg, i)) if rm_fn else None)[0]
                    def post(O=O, qg=qg):
                        ob = obr.next()
                        normalize(O, ob.t, ob)
                        store(ob, orow, qg)
                    pending.append(post)

            k.dma("sp", b_t[:, 8:32, :], self.rmc[:, :, :].rearrange("n p t -> p n t"), writes=bias[8:32])
            for h in range(4):
                load_head(64 * h, 64 * h, True)
                for j in range(8):
                    for krl in range(2):
                        a0 = 15 - 2 * j - krl
                        src = self.natab[((l * 4 + h) * 23 + a0) * 64:((l * 4 + h) * 23 + a0 + 8) * 64, :].rearrange("(a kc) qc -> kc a qc", kc=64)
                        k.dma("sp", b_t[krl * 64:(krl + 1) * 64, j, :].rearrange("p (a q) -> p a q", a=8), src,
                              reads=[self.natab], writes=[bias[j]], acc=(krl == 1))
                simple_head(64 * h, lambda qg: [6 + 4 * qg + j for j in range(8)], 0.125,
                            bias_fn=lambda qg, i: bias[i],
                            rm_fn=lambda qg, i: bias[8 + (0 if qg == 0 else (2 if qg == NG - 1 else 1)) * 8 + i])
            k.dma("sp", b_t[:, 0:20, :], self.dilb[:, :, :].rearrange("n p t -> p n t"), writes=bias[0:20])
            for h in range(4):
                load_head(640 + 64 * h, 768 + 64 * h, True)
                simple_head(768 + 64 * h, lambda qg: [4 * qg + i for i in range(20)], 0.125, bias_fn=lambda qg, i: bias[i])
            k.op("pool", lambda e: e.memset(va_t[:, :, 64:128], 1.0), reads=[vab], writes=[vones])
            allc = list(range(64))
            for n_ in range(2):
                load_head(512 + 64 * n_, 512 + 128 * n_, False, pair=True)
                for qg in range(NG):
                    Os = attend(qg, allc, 0.125, [(0, 64), (64, 64)])

                    def post(Os=Os, qg=qg, n_=n_):
                        for mi in range(2):
                            ob = obr.next()
                            normalize(Os[mi], ob.t, ob)
                            store(ob, 512 + 128 * n_ + 64 * mi, qg)
                    pending.append(post)
            dsc = 32 ** -0.5
            flush_pending()
            k.op("pool", lambda e: e.memset(q_t[32:64, :], 0.0), writes=[qb])
            k.op("pool", lambda e: e.memset(q_t[64:96, :], 0.0), reads=[qb], writes=[qb])
            qzero = Buf("qzero")
            qzero.w = list(qb.w)
            for h in range(4):
                krow = 256 + 64 * h
                qc, qp = krow // 128, krow % 128
                k.dma("sp", q_t[0:32, :], qt[qc, qp:qp + 32, :], reads=[qt, qzero], writes=[qb])
                k.dma("sp", q_t[96:128, :], qt[qc, qp + 32:qp + 64, :], reads=[qt], writes=[qb], acc=True)
                for r in range(2):
                    k.dma("sp", kt_t[0:64, r * TOK:(r + 1) * TOK], ktg[r * 896 + krow:r * 896 + krow + 64, :], reads=[ktg], writes=[ktb], acc=(r == 1))
                    k.dma("sp", kt_t[64:128, r * TOK:(r + 1) * TOK], ktg[r * 896 + krow:r * 896 + krow + 64, :], reads=[ktg], writes=[ktb], acc=True)
                    k.dma("sp", va_t[:, r * 32:(r + 1) * 32, 0:64], vg[r * TOK:(r + 1) * TOK, krow:krow + 64].rearrange("(c p) d -> p c d", p=128),
                          reads=[vg], writes=[vab], acc=(r == 1))
                for qg in range(NG):
                    Os = attend(qg, allc, dsc, [(0, 64), (64, 64)])

                    def post(Os=Os, qg=qg, h=h):
                        o12 = []
                        for mp in range(2):
                            ob_ = postr.next()
                            normalize(Os[mp], ob_.t[0:64, :], ob_)
                            o12.append(ob_)
                        od = postr.next()
                        k.op("dve", lambda e: e.scalar_tensor_tensor(out=od.t[0:64, :], in0=o12[1].t[0:64, :], scalar=sm_t[:, 134:135], in1=o12[0].t[0:64, :],
                                                                     op0=ALU.mult, op1=ALU.add), reads=[o12[0], o12[1], smb], writes=[od])
                        sq = pr.next()
                        k.op("pool", lambda e: e.tensor_tensor(out=sq.t[0:64, :], in0=od.t[0:64, :], in1=od.t[0:64, :], op=ALU.mult), reads=[od], writes=[sq])
                        ss = m_ring.next()
                        k.op("pe", lambda e: e.matmul(ss.t[0:64, :], lhsT=self.bd[0:64, 128:192], rhs=sq.t[0:64, :], start=True, stop=True),
                             reads=[sq, self.bd], writes=[ss])
                        ln = postr.next()
                        k.op("act", lambda e: e.activation(out=ln.t[0:64, :], in_=ss.t[0:64, :], func=AF.Ln, scale=1.0 / 64, bias=EPS), reads=[ss], writes=[ln])
                        k.op("act", lambda e: e.activation(out=ln.t[0:64, :], in_=ln.t[0:64, :], func=AF.Exp, scale=-0.5), reads=[ln], writes=[ln])
                        ob = obr.next()
                        k.op("dve", lambda e: e.scalar_tensor_tensor(out=ob.t, in0=od.t[0:64, :], scalar=sm_t[:, 135:136], in1=ln.t[0:64, :],
                                                                     op0=ALU.mult, op1=ALU.mult), reads=[od, ln, smb], writes=[ob])
                        store(ob, 256 + 64 * h, qg)
                    pending.append(post)
            flush_pending()
            allb = [ktb, vab, vones, qb, smb, vpnb] + pr.bufs + sr.bufs + bias + postr.bufs + obr.bufs
            for e in ("sp", "pool", "act", "dve", "pe"):
                k.wait_all(e, allb)


def _core_inputs(inp):
    wall = prep_weights(inp)
    vecs, lam, natab = prep_vecs(inp)
    dilb = dil_bias()
    bdc = block_consts()
    rm_first, rm_int, rm_last = na_rowmask(0), na_rowmask(16), na_rowmask(120)
    per_core = []
    for c in range(8):
        b, half = c // 2, c % 2
        xT = np.ascontiguousarray(inp["x"][b, half * TOK:(half + 1) * TOK, :].T).reshape(8, 128, TOK)
        rmc = np.stack([rm_first if half == 0 else rm_int, rm_int, rm_last if half == 1 else rm_int]).reshape(24, 128, 512)
        vpn = np.zeros((128, 2), np.float32)
        vpn[:, 0] = 1.0 if half == 1 else 0.0
        vpn[:, 1] = 1.0 if half == 0 else 0.0
        per_core.append({"x_in": xT, "vecs": vecs, "lam": lam, "natab": natab.reshape(-1, 64),
                         "tabs": rope_tables_core(half).reshape(NG, 128, 6 * G), "rmc": rmc, "dilb": dilb,
                         "bdc": bdc, "vpn": vpn})
    return wall, per_core


def kernel(**inputs):
    inp = {k_: np.asarray(v) for k_, v in inputs.items()}
    wall, per_core = _core_inputs(inp)
    lam_all = per_core[0]["lam"]
    lam_pack = np.zeros((64, L * 130), np.float32)
    for l in range(L):
        lam_init = 0.8 - 0.6 * math.exp(-0.3 * l)
        lam_pack[:, l * 130:l * 130 + 128] = lam_all[:, l * 128:(l + 1) * 128]
        lam_pack[:, l * 130 + 128] = -lam_init
        lam_pack[:, l * 130 + 129] = 1.0 - lam_init
    stages = []
    for l in range(L):
        stages += [("TA", l), ("ATT", l), ("TB", l)]
    prog = Prog(stages, exchange="cc")
    maps = []
    for c in range(8):
        pc = per_core[c]
        maps.append({"x_in": pc["x_in"], "wall": wall, "vecs": pc["vecs"], "bdc": pc["bdc"], "tabs": pc["tabs"],
                     "lam": lam_pack, "natab": pc["natab"], "rmc": pc["rmc"], "dilb": pc["dilb"], "vpn": pc["vpn"]})
    res = run_bass_kernel_spmd(prog.nc, maps, core_ids=list(range(8)))
    out = np.empty((4, S, D), np.float32)
    for c in range(8):
        b, half = c // 2, c % 2
        out[b, half * TOK:(half + 1) * TOK, :] = np.asarray(res.results[c]["x_out"], np.float32).reshape(D, TOK).T
    return out
```
